# Optimizing a Trainium2 kernel written in Bass

```python
import jax
import jax.numpy as jnp
from jax import lax
import numpy as np

D_MODEL = 1024
BATCH = 2
SEQ = 8192
DEPTH = 2

EPS = 1e-6
ML_HEADS = 4
ML_QK = 64
ML_V = 128
ML_CONV = 4
ML_CHUNK = 64
MLA_HEADS = 8
MLA_NOPE = 64
MLA_ROPE = 32
MLA_V = 64
MLA_Q_RANK = 256
MLA_KV_RANK = 128
ROPE_THETA = 10000.0
Q_BLOCK = 128
D_FF = 4 * D_MODEL

ML_WIDTH = ML_HEADS * ML_V
MLA_WIDTH = MLA_HEADS * MLA_V
MIX_WIDTH = ML_WIDTH + MLA_WIDTH
QK_COLS = 2 * ML_HEADS * ML_QK
GATE_COLS = 2 * ML_HEADS
IN_WIDTH = QK_COLS + 2 * ML_WIDTH + GATE_COLS + MLA_Q_RANK + MLA_KV_RANK + MLA_ROPE

kernel_name = 'hybrid_mlstm_mla_sandwich'


def rmsnorm(x, g):
    xf = x.astype(jnp.float32)
    y = xf * lax.rsqrt(jnp.mean(xf * xf, axis=-1, keepdims=True) + EPS)
    return (y * g.astype(jnp.float32)).astype(x.dtype)


def rope_tables(positions):
    inv = 1.0 / (ROPE_THETA ** (jnp.arange(0, MLA_ROPE, 2, dtype=jnp.float32) / MLA_ROPE))
    ang = positions.astype(jnp.float32)[..., None] * inv
    return jnp.cos(ang), jnp.sin(ang)


def apply_rope(x, cos, sin):
    xf = x.astype(jnp.float32)
    x1, x2 = jnp.split(xf, 2, axis=-1)
    return jnp.concatenate([x1 * cos - x2 * sin, x1 * sin + x2 * cos], axis=-1).astype(x.dtype)


def causal_conv(x, w, b):
    K = w.shape[0]
    S = x.shape[1]
    xp = jnp.pad(x, ((0, 0), (K - 1, 0), (0, 0)))
    return sum(xp[:, k:k + S] * w[k] for k in range(K)) + b


def mlstm_chunkwise(q, k, v, i_pre, f_pre):
    B, S, H, dk = q.shape
    dv = v.shape[-1]
    L = ML_CHUNK
    NC = S // L
    f32 = jnp.float32

    def chunk(t):
        t = t.reshape((B, NC, L, H) + t.shape[3:])
        return jnp.moveaxis(t, 3, 1)

    qc = chunk(q).astype(f32)
    kc = chunk(k).astype(f32) * (dk ** -0.5)
    vc = chunk(v).astype(f32)
    ig = chunk(i_pre).astype(f32)
    logf = jax.nn.log_sigmoid(chunk(f_pre).astype(f32))
    b = jnp.cumsum(logf, axis=-1)
    b_end = b[..., -1]

    a = b_end[..., None] - b + ig
    m_loc = jnp.max(a, axis=-1)
    w_loc = jnp.exp(a - m_loc[..., None])
    C_loc = jnp.einsum('bhcl,bhcld,bhcle->bhcde', w_loc, kc, vc)
    n_loc = jnp.einsum('bhcl,bhcld->bhcd', w_loc, kc)

    def step(carry, xs):
        C, n, m = carry
        Cl, nl, ml, bl = xs
        m_new = jnp.maximum(bl + m, ml)
        s_prev = jnp.exp(bl + m - m_new)
        s_loc = jnp.exp(ml - m_new)
        C_new = s_prev[..., None, None] * C + s_loc[..., None, None] * Cl
        n_new = s_prev[..., None] * n + s_loc[..., None] * nl
        return (C_new, n_new, m_new), (C, n, m)

    init = (jnp.zeros((B, H, dk, dv), f32), jnp.zeros((B, H, dk), f32), jnp.zeros((B, H), f32))
    xs = (jnp.moveaxis(C_loc, 2, 0), jnp.moveaxis(n_loc, 2, 0),
          jnp.moveaxis(m_loc, 2, 0), jnp.moveaxis(b_end, 2, 0))
    _, (C_prev, n_prev, m_prev) = lax.scan(step, init, xs)
    C_prev = jnp.moveaxis(C_prev, 0, 2)
    n_prev = jnp.moveaxis(n_prev, 0, 2)
    m_prev = jnp.moveaxis(m_prev, 0, 2)

    causal = jnp.tril(jnp.ones((L, L), dtype=bool))
    D = b[..., :, None] - b[..., None, :] + ig[..., None, :]
    D = jnp.where(causal, D, -jnp.inf)
    m_inter = b + m_prev[..., None]
    m_comb = jnp.maximum(jnp.max(D, axis=-1), m_inter)
    Dw = jnp.exp(D - m_comb[..., None])
    inter_w = jnp.exp(m_inter - m_comb)
    s = jnp.einsum('bhcid,bhcjd->bhcij', qc, kc) * Dw
    num = jnp.einsum('bhcij,bhcje->bhcie', s, vc) + inter_w[..., None] * jnp.einsum('bhcid,bhcde->bhcie', qc, C_prev)
    den = jnp.sum(s, axis=-1) + inter_w * jnp.einsum('bhcid,bhcd->bhci', qc, n_prev)
    h = num / jnp.maximum(jnp.abs(den), jnp.exp(-m_comb))[..., None]
    return jnp.moveaxis(h, 1, 3).reshape(B, S, H, dv)


def causal_block_attention(q, k, v):
    B, S, H, dq = q.shape
    dv = v.shape[-1]
    nb = S // Q_BLOCK
    scale = dq ** -0.5
    kf = k.astype(jnp.float32)
    vf = v.astype(jnp.float32)
    qb = jnp.transpose(q.reshape(B, nb, Q_BLOCK, H, dq), (1, 0, 3, 2, 4))
    kpos = jnp.arange(S)

    def one_block(args):
        qblk, bi = args
        sc = jnp.einsum('bhqd,bkhd->bhqk', qblk.astype(jnp.float32), kf) * scale
        qpos = bi * Q_BLOCK + jnp.arange(Q_BLOCK)
        sc = jnp.where(kpos[None, :] <= qpos[:, None], sc, -jnp.inf)
        p = jax.nn.softmax(sc, axis=-1)
        return jnp.einsum('bhqk,bkhd->bqhd', p, vf)

    out = lax.map(one_block, (qb, jnp.arange(nb)))
    return jnp.transpose(out, (1, 0, 2, 3, 4)).reshape(B, S, H * dv)


def token_mixer(a, cos, sin, w_in, b_gates, conv_w, conv_b, ml_head_norm,
                q_norm, w_uq, kv_norm, w_ukv, w_out):
    B, S, _ = a.shape
    proj = a @ w_in
    o0 = QK_COLS
    o1 = o0 + ML_WIDTH
    o2 = o1 + ML_WIDTH
    o3 = o2 + GATE_COLS
    o4 = o3 + MLA_Q_RANK
    o5 = o4 + MLA_KV_RANK
    qk_ml = proj[..., :o0]
    v_ml = proj[..., o0:o1]
    o_ml = proj[..., o1:o2]
    gates = proj[..., o2:o3] + b_gates
    c_q = proj[..., o3:o4]
    c_kv = proj[..., o4:o5]
    k_r = proj[..., o5:]

    qk_ml = jax.nn.silu(causal_conv(qk_ml, conv_w, conv_b))
    q_ml = qk_ml[..., :QK_COLS // 2].reshape(B, S, ML_HEADS, ML_QK)
    k_ml = qk_ml[..., QK_COLS // 2:].reshape(B, S, ML_HEADS, ML_QK)
    h_ml = mlstm_chunkwise(q_ml, k_ml, v_ml.reshape(B, S, ML_HEADS, ML_V),
                           gates[..., :ML_HEADS], gates[..., ML_HEADS:])
    h_ml = rmsnorm(h_ml, ml_head_norm.reshape(ML_HEADS, ML_V))
    h_ml = h_ml * jax.nn.sigmoid(o_ml.astype(jnp.float32)).reshape(B, S, ML_HEADS, ML_V)
    h_ml = h_ml.reshape(B, S, ML_WIDTH).astype(a.dtype)

    q = (rmsnorm(c_q, q_norm) @ w_uq).reshape(B, S, MLA_HEADS, MLA_NOPE + MLA_ROPE)
    q = jnp.concatenate([q[..., :MLA_NOPE],
                         apply_rope(q[..., MLA_NOPE:], cos[:, :, None, :], sin[:, :, None, :])], axis=-1)
    kv = (rmsnorm(c_kv, kv_norm) @ w_ukv).reshape(B, S, MLA_HEADS, MLA_NOPE + MLA_V)
    k_rope = apply_rope(k_r, cos, sin)[:, :, None, :]
    k = jnp.concatenate([kv[..., :MLA_NOPE],
                         jnp.broadcast_to(k_rope, (B, S, MLA_HEADS, MLA_ROPE))], axis=-1)
    h_mla = causal_block_attention(q, k, kv[..., MLA_NOPE:]).astype(a.dtype)

    return jnp.concatenate([h_ml, h_mla], axis=-1) @ w_out


def setup_inputs(seed: int = 0) -> dict:
    key = jax.random.key(seed)
    ks = jax.random.split(key, 20)

    def nrm(k, shape, scale):
        return jax.random.normal(k, shape, jnp.float32) * scale

    def gain(k, shape):
        return 1.0 + 0.05 * jax.random.normal(k, shape, jnp.float32)

    x = nrm(ks[0], (BATCH, SEQ, D_MODEL), 1.0)
    positions = jnp.broadcast_to(jnp.arange(SEQ, dtype=jnp.int32), (BATCH, SEQ))
    b_gates = jnp.concatenate([nrm(ks[3], (DEPTH, ML_HEADS), 0.1),
                               3.0 + nrm(ks[4], (DEPTH, ML_HEADS), 0.1)], axis=-1)
    return {
        'x': x,
        'positions': positions,
        'norm_pre_mix': gain(ks[1], (DEPTH, D_MODEL)),
        'w_in': nrm(ks[2], (DEPTH, D_MODEL, IN_WIDTH), D_MODEL ** -0.5),
        'b_gates': b_gates,
        'conv_w': nrm(ks[5], (DEPTH, ML_CONV, QK_COLS), ML_CONV ** -0.5),
        'conv_b': nrm(ks[6], (DEPTH, QK_COLS), 0.02),
        'ml_head_norm': gain(ks[7], (DEPTH, ML_WIDTH)),
        'q_norm': gain(ks[8], (DEPTH, MLA_Q_RANK)),
        'w_uq': nrm(ks[9], (DEPTH, MLA_Q_RANK, MLA_HEADS * (MLA_NOPE + MLA_ROPE)), MLA_Q_RANK ** -0.5),
        'kv_norm': gain(ks[10], (DEPTH, MLA_KV_RANK)),
        'w_ukv': nrm(ks[11], (DEPTH, MLA_KV_RANK, MLA_HEADS * (MLA_NOPE + MLA_V)), MLA_KV_RANK ** -0.5),
        'w_out': nrm(ks[12], (DEPTH, MIX_WIDTH, D_MODEL), MIX_WIDTH ** -0.5),
        'norm_post_mix': gain(ks[13], (DEPTH, D_MODEL)),
        'norm_pre_mlp': gain(ks[14], (DEPTH, D_MODEL)),
        'w_up': nrm(ks[15], (DEPTH, D_MODEL, D_FF), D_MODEL ** -0.5),
        'w_down': nrm(ks[16], (DEPTH, D_FF, D_MODEL), D_FF ** -0.5),
        'norm_post_mlp': gain(ks[17], (DEPTH, D_MODEL)),
    }


def reference(x, positions, norm_pre_mix, w_in, b_gates, conv_w, conv_b, ml_head_norm,
              q_norm, w_uq, kv_norm, w_ukv, w_out, norm_post_mix, norm_pre_mlp,
              w_up, w_down, norm_post_mlp):
    cos, sin = rope_tables(positions)
    for l in range(DEPTH):
        a = rmsnorm(x, norm_pre_mix[l])
        mix = token_mixer(a, cos, sin, w_in[l], b_gates[l], conv_w[l], conv_b[l], ml_head_norm[l],
                          q_norm[l], w_uq[l], kv_norm[l], w_ukv[l], w_out[l])
        x = x + rmsnorm(mix, norm_post_mix[l])
        m = rmsnorm(x, norm_pre_mlp[l])
        y = jnp.square(jax.nn.relu(m @ w_up[l])) @ w_down[l]
        x = x + rmsnorm(y, norm_post_mlp[l])
    return x
```

```python
import contextlib
import math

import ml_dtypes
import numpy as np

import concourse.bass as bass
import concourse.mybir as mybir
from concourse.bass_utils import run_bass_kernel_spmd

F32 = mybir.dt.float32
BF16 = mybir.dt.bfloat16
I32 = mybir.dt.int32
AF = mybir.ActivationFunctionType
ALU = mybir.AluOpType

D = 1024
S = 8192
NL = 2
TOK = 2048
NBLK = 16
DFF = 4096
EPS = 1e-6
NPP = 40
QSCALE = 96 ** -0.5
LN8 = math.log(0.125)
GROUPS = [[0, 1, 2, 3], [4, 5, 6, 7]]
ARENA = 121344
ARENA0 = 22752
CR = 4


class _Op:
    __slots__ = ("eng", "fn", "deps", "dma_key", "inc", "signal", "token", "nobar")

    def __init__(self, eng, fn, dma_key, inc):
        self.eng = eng
        self.fn = fn
        self.deps = []
        self.dma_key = dma_key
        self.inc = inc
        self.signal = False
        self.token = None


class Prog:
    ENGS = ("pe", "act", "dve", "pool", "sp")

    def __init__(self, nc):
        self.nc = nc
        self.ops = {e: [] for e in self.ENGS}
        self.lastw = {}
        self.readers = {}
        self.dma_cnt = {}
        self.pending = {e: [] for e in self.ENGS}
        self.dmas_since = []

    def add(self, eng, fn, r=(), w=(), dma_key=None, inc=16, nobar=False):
        op = _Op(eng, fn, dma_key, inc)
        op.nobar = nobar
        deps = {}
        for x in r:
            p = self.lastw.get(x)
            if p is not None:
                deps[id(p)] = (p, True)
        for x in w:
            p = self.lastw.get(x)
            if p is not None and id(p) not in deps:
                deps[id(p)] = (p, False)
            for rd in self.readers.get(x, ()):
                if id(rd) not in deps:
                    deps[id(rd)] = (rd, False)
        for x in r:
            lst = self.readers.setdefault(x, [])
            if dma_key is None:
                lst[:] = [o for o in lst if not (o.dma_key is None and o.eng == eng)]
            lst.append(op)
        for x in w:
            self.lastw[x] = op
            self.readers[x] = []
        for p in self.pending[eng]:
            if id(p) not in deps:
                deps[id(p)] = (p, True)
        self.pending[eng] = []
        for dep, raw in deps.values():
            if dep is op:
                continue
            if dep.dma_key is not None:
                op.deps.append(dep)
            elif dep.eng != eng:
                dep.signal = True
                op.deps.append(dep)
            elif eng != "pe":
                dep.signal = True
                op.deps.append(dep)
        if dma_key is not None:
            c = self.dma_cnt.get(dma_key, 0) + inc
            self.dma_cnt[dma_key] = c
            op.token = (("dma", dma_key), c)
            if not nobar:
                self.dmas_since.append(op)
        self.ops[eng].append(op)
        return op

    def barrier(self):
        last = []
        for e in self.ENGS:
            for o in reversed(self.ops[e]):
                if not o.nobar:
                    last.append(o)
                    break
        allp = last + self.dmas_since
        self.dmas_since = []
        for e in self.ENGS:
            self.pending[e] = list(allp)

    def emit(self, stack):
        nc = self.nc
        sems = {}
        for e in self.ENGS:
            sems[("eng", e)] = stack.enter_context(nc.semaphore("s_" + e))
        for i, k in enumerate(self.dma_cnt):
            sems[("dma", k)] = stack.enter_context(nc.semaphore("d%d" % i))
        for e in self.ENGS:
            n = 0
            for op in self.ops[e]:
                if op.dma_key is None and op.signal:
                    n += 1
                    op.token = (("eng", e), n)
        block = stack.enter_context(nc.Block())
        handles = {"pe": block.tensor, "act": block.scalar, "dve": block.vector,
                   "pool": block.gpsimd, "sp": block.sync}
        for e in self.ENGS:
            ops = self.ops[e]

            def body(eng, ops=ops):
                waited = {}
                for op in ops:
                    need = {}
                    for d in op.deps:
                        k, v = d.token
                        if v > waited.get(k, 0) and v > need.get(k, 0):
                            need[k] = v
                    for k, v in need.items():
                        eng.wait_ge(sems[k], v)
                        waited[k] = v
                    ins = op.fn(eng)
                    if op.dma_key is not None:
                        ins.then_inc(sems[op.token[0]], op.inc)
                    elif op.signal:
                        ins.then_inc(sems[op.token[0]], 1)
                final = {}
                for op in ops:
                    if op.dma_key is not None:
                        k, v = op.token
                        final[k] = max(final.get(k, 0), v)
                for k, v in final.items():
                    if v > waited.get(k, 0):
                        eng.wait_ge(sems[k], v)
                        waited[k] = v

            handles[e](body)


class Arena:
    def __init__(self, ap, nbytes):
        self.ap = ap
        self.nbytes = nbytes
        self.off = 0

    def reset(self):
        self.off = 0

    def alloc(self, shape, dtype):
        item = 4 if dtype in (F32, I32) else 2
        n = 1
        for s_ in shape:
            n *= s_
        sz = (n * item + 31) // 32 * 32
        assert self.off + sz <= self.nbytes, ("arena overflow", self.off, sz)
        a = self.ap[:, self.off // 4:(self.off + sz) // 4]
        self.off += sz
        if dtype != F32:
            a = a.bitcast(dtype)
        a = a[:, 0:n]
        if len(shape) == 2:
            a = a.rearrange("p (a b) -> p a b", a=shape[0], b=shape[1])
        elif len(shape) == 3:
            a = a.rearrange("p (a b c) -> p a b c", a=shape[0], b=shape[1], c=shape[2])
        return a


class _Stop(Exception):
    pass


def build_program(dbg=(), maxstage=99, sim=False, cut=99, gcut=99):
    nc = bass.Bass("TRN2", target_bir_lowering=False)

    def din(name, shape, dt):
        return nc.dram_tensor(name, shape, dt, kind="ExternalInput").ap()

    x_d = din("x", [TOK, D], F32)
    pos_d = din("pos32", [32, S], I32)
    winA_d = din("w_inA", [NL, D, 386], F32)
    winB_d = din("w_inB", [NL, D, 448], F32)
    wuq_d = din("w_uq", [NL, 3, 128, 384], F32)
    wukv_d = din("w_ukv", [NL, 128, 256], F32)
    wout_d = din("w_out", [NL, D, D], F32)
    wup_d = din("w_up", [NL, D, DFF], F32)
    wdn_d = din("w_down", [NL, DFF, D], F32)
    pp_d = din("pp", [NL, 128, NPP], F32)
    gml_d = din("gml", [NL, 128, 128], F32)
    gpost_d = din("gpost", [NL, 2, 128, D], F32)
    identb_d = din("identb", [128, 128], BF16)
    negmask_d = din("negmask", [128, 128], BF16)
    onesb_d = din("onesb", [128, 128], BF16)
    U_d = din("Umat", [128, 128], F32)
    onesf_d = din("onesf", [128, 128], F32)
    gidx_d = din("gidx", [128, 8], I32)
    out_d = nc.dram_tensor("out", [TOK, D], F32, kind="ExternalOutput").ap()
    dbg_d = {}
    for name, shape, dt in dbg:
        dbg_d[name] = nc.dram_tensor(name, shape, dt, kind="ExternalOutput").ap()

    agin = nc.dram_tensor("agin", [4, D, 512], BF16)
    if sim:
        agout = nc.dram_tensor("agout", [4, 4 * D, 512], BF16, kind="ExternalInput")
    else:
        agout = nc.dram_tensor("agout", [4, 4 * D, 512], BF16)
    hsend = nc.dram_tensor("hsend", [4, 256, TOK], BF16)
    if sim:
        hall = nc.dram_tensor("hall", [4, 1024, TOK], BF16, kind="ExternalInput")
    else:
        hall = nc.dram_tensor("hall", [4, 1024, TOK], BF16)
    ropeC = nc.dram_tensor("ropeC", [32, S], F32)
    ropeS = nc.dram_tensor("ropeS", [32, S], F32)

    st = contextlib.ExitStack()
    with st:
        def T(n, s_, d_):
            return st.enter_context(nc.sbuf_tensor(n, s_, d_))

        X = T("X", [128, NBLK, D], F32)
        identb = T("identb_s", [128, 128], BF16)
        negmask = T("negmask_s", [128, 128], BF16)
        onesb = T("onesb_s", [128, 128], BF16)
        Um = T("U_s", [128, 128], F32)
        onesf = T("onesf_s", [128, 128], F32)
        gidx = T("gidx_s", [128, 8], I32)
        pp = T("pp_s", [128, NL, NPP], F32)
        gml = T("gml_s", [128, NL, 128], F32)
        nbf = T("nbf_s", [128, NL], F32)
        arena_t = T("arena", [128, (ARENA + ARENA0) // 4], F32)
        ps = st.enter_context(nc.psum_tensor("ps", [128, 8, 512], F32))
        ar = Arena(arena_t[:, 0:ARENA // 4], ARENA)
        ar0 = Arena(arena_t[:, ARENA // 4:(ARENA + ARENA0) // 4], ARENA0)
        P = Prog(nc)
        A = P.add

        bank_ctr = [0]

        def nb(pool=(0, 1, 2, 3, 4, 5)):
            b = pool[bank_ctr[0] % len(pool)]
            bank_ctr[0] += 1
            return b

        def PS(b):
            return ("ps", b)

        for i, (dst, src) in enumerate([(identb, identb_d), (negmask, negmask_d), (onesb, onesb_d),
                                         (Um, U_d), (onesf, onesf_d), (gidx, gidx_d)]):
            A("sp", lambda e, dst=dst, src=src: e.dma_start(out=dst[:], in_=src), w=[("c", i)], dma_key=("const", i))
        A("sp", lambda e: e.dma_start(out=pp[:], in_=pp_d.rearrange("l p n -> p l n")), w=["pp"], dma_key=("const", "pp"))
        A("sp", lambda e: e.dma_start(out=gml[:], in_=gml_d.rearrange("l p n -> p l n")), w=["gml"], dma_key=("const", "gml"))
        CONSTS = [("c", i) for i in range(6)] + ["pp", "gml"]
        xv = x_d.rearrange("(i p) d -> p i d", p=128)
        for j in range(4):
            A("sp", lambda e, j=j: e.dma_start(out=X[:, 4 * j:4 * j + 4, :], in_=xv[:, 4 * j:4 * j + 4, :]),
              w=[("X", i) for i in range(4 * j, 4 * j + 4)], dma_key=("x", j))
        A("dve", lambda e: e.tensor_scalar(out=nbf[:], in0=pp[:, :, 20], scalar1=-1.0, scalar2=None, op0=ALU.mult),
          r=["pp"], w=["nbf"])

        def dbg_dump(name, src_ap, res):
            if name in dbg_d:
                A("sp", lambda e: e.dma_start(out=dbg_d[name], in_=src_ap), r=res, dma_key=("dbg", name))

        ar0.reset()
        aTs = ar0.alloc([2, 8, 512], BF16)
        junk0 = ar0.alloc([D], BF16)
        ssq0 = ar0.alloc([NBLK], F32)
        rsd0 = ar0.alloc([NBLK], F32)
        abf0 = ar0.alloc([2, D], BF16)

        def norm_T(blocks, dst_fn, ssq=ssq0, rsd=rsd0, abf=abf0, junk=junk0):
            for ii, i in enumerate(blocks):
                sl = ii % 2
                A("act", lambda e, i=i: e.activation(out=junk[:], in_=X[:, i, :], func=AF.Square, accum_out=ssq[:, i:i + 1], scale=1.0),
                  r=[("X", i)], w=["junk", ("ssq", i)])
                A("act", lambda e, i=i: e.activation(out=rsd[:, i:i + 1], in_=ssq[:, i:i + 1], func=AF.Ln, bias=EPS, scale=1.0 / D),
                  r=[("ssq", i)], w=[("rsd", i)])
                A("act", lambda e, i=i: e.activation(out=rsd[:, i:i + 1], in_=rsd[:, i:i + 1], func=AF.Exp, scale=-0.5),
                  r=[("rsd", i)], w=[("rsd", i)])
                A("dve", lambda e, i=i, sl=sl: e.tensor_scalar(out=abf[:, sl, :], in0=X[:, i, :], scalar1=rsd[:, i:i + 1], scalar2=None, op0=ALU.mult),
                  r=[("X", i), ("rsd", i)], w=[("abf", id(abf), sl)])
                for half in range(2):
                    b = nb()
                    for c4 in range(4):
                        c = half * 4 + c4
                        A("pe", lambda e, b=b, c4=c4, c=c, sl=sl: e.matmul(ps[:, b, c4 * 128:(c4 + 1) * 128], lhsT=abf[:, sl, c * 128:(c + 1) * 128],
                                                                         rhs=identb[:], start=True, stop=True),
                          r=[("abf", id(abf), sl), ("c", 0)], w=[PS(b)])
                    dst, res = dst_fn(ii, half)
                    if half == 0:
                        A("act", lambda e, b=b, dst=dst: e.activation(out=dst, in_=ps[:, b, :].rearrange("p (c t) -> p c t", c=4), func=AF.Copy, scale=1.0),
                          r=[PS(b)], w=[res])
                    else:
                        A("dve", lambda e, b=b, dst=dst: e.tensor_copy(out=dst, in_=ps[:, b, :].rearrange("p (c t) -> p c t", c=4)),
                          r=[PS(b)], w=[res])

        aginv = agin.ap().rearrange("q (c p) t -> q p c t", p=128)

        def emit_P0(l, q4s):
            for q4 in q4s:
                sl = q4 % 2
                norm_T(range(4 * q4, 4 * q4 + 4),
                       lambda ii, half, sl=sl: (aTs[:, sl, half * 4:(half + 1) * 4, ii * 128:(ii + 1) * 128], ("aTs", sl, ii, half)))
                A("sp", lambda e, q4=q4, sl=sl: e.dma_start(out=aginv[q4], in_=aTs[:, sl]),
                  r=[("aTs", sl, ii, h) for ii in range(4) for h in range(2)], w=[("agin", q4)], dma_key=("agin", sl))
                if not sim:
                    A("pool", lambda e, q4=q4: e.collective_compute("AllGather", ALU.bypass, replica_groups=GROUPS,
                                                                    ins=[agin.ap()[q4].opt()], outs=[agout.ap()[q4].opt()]),
                      r=[("agin", q4)], w=[("agout", q4)], dma_key=("cc1", q4), inc=1, nobar=True)
            if l == 0 and list(q4s)[-1] == 3:
                dbg_dump("d_agout", agout.ap().rearrange("q r t -> (q r) t"), [("agout", q) for q in range(4)])

        if maxstage >= 0:
            emit_P0(0, range(4))

        ar.reset()
        posi = ar.alloc([512], I32)
        posf = ar.alloc([512], F32)
        ang = ar.alloc([512], F32)
        ang2 = ar.alloc([512], F32)
        tabc = ar.alloc([2, 512], F32)
        tabs = ar.alloc([2, 512], F32)
        rk = ar.alloc([512], F32)
        rki = ar.alloc([512], I32)
        rr1 = ar.alloc([512], F32)
        CW1 = 6.28125
        CW2 = 2.0 * math.pi - 6.28125
        PI_SAFE = 3.1415925
        R6 = slice(64, 96)
        inv_ap = pp[R6, 0, 31:32]
        sgn_ap = pp[R6, 0, 32:33]
        TWO_PI = 2.0 * math.pi
        for n in range(16):
            sl = n % 2
            A("sp", lambda e, n=n: e.dma_start(out=posi[R6, :], in_=pos_d[:, n * 512:(n + 1) * 512]),
              w=["posi"], dma_key="posi")
            A("dve", lambda e: e.tensor_copy(out=posf[R6, :], in_=posi[R6, :]), r=["posi"], w=["posf"])
            A("dve", lambda e: e.tensor_scalar(out=ang[R6, :], in0=posf[R6, :], scalar1=inv_ap, scalar2=None, op0=ALU.mult),
              r=["posf", "pp"], w=["ang"])
            for (tab, nm, shift) in ((tabs, "tabs", 0.0), (tabc, "tabc", 0.5 * math.pi)):
                src = ang
                if shift != 0.0:
                    A("dve", lambda e, shift=shift: e.tensor_scalar(out=ang2[R6, :], in0=ang[R6, :], scalar1=shift, scalar2=None, op0=ALU.add),
                      r=["ang"], w=["ang2"])
                    src = ang2
                sres = "ang" if shift == 0.0 else "ang2"
                A("dve", lambda e, src=src: e.tensor_scalar(out=rk[R6, :], in0=src[R6, :], scalar1=1.0 / TWO_PI, scalar2=None, op0=ALU.mult),
                  r=[sres], w=["rk"])
                A("dve", lambda e: e.tensor_copy(out=rki[R6, :], in_=rk[R6, :]), r=["rk"], w=["rki"])
                A("dve", lambda e: e.tensor_copy(out=rk[R6, :], in_=rki[R6, :]), r=["rki"], w=["rk"])
                A("dve", lambda e, src=src: e.scalar_tensor_tensor(out=rr1[R6, :], in0=rk[R6, :], scalar=-CW1, in1=src[R6, :], op0=ALU.mult, op1=ALU.add),
                  r=["rk", sres], w=["rr1"])
                A("dve", lambda e: e.scalar_tensor_tensor(out=rr1[R6, :], in0=rk[R6, :], scalar=-CW2, in1=rr1[R6, :], op0=ALU.mult, op1=ALU.add),
                  r=["rk", "rr1"], w=["rr1"])
                A("dve", lambda e: e.tensor_scalar(out=rr1[R6, :], in0=rr1[R6, :], scalar1=-PI_SAFE, scalar2=PI_SAFE, op0=ALU.max, op1=ALU.min),
                  r=["rr1"], w=["rr1"])
                A("act", lambda e, tab=tab, sl=sl: e.activation(out=tab[R6, sl, :], in_=rr1[R6, :], func=AF.Sin, scale=1.0),
                  r=["rr1"], w=[(nm, sl)])
            A("dve", lambda e, sl=sl: e.tensor_scalar(out=tabs[R6, sl, :], in0=tabs[R6, sl, :], scalar1=sgn_ap, scalar2=None, op0=ALU.mult),
              r=[("tabs", sl), "pp"], w=[("tabs", sl)])
            A("sp", lambda e, n=n, sl=sl: e.dma_start(out=ropeC.ap()[:, n * 512:(n + 1) * 512], in_=tabc[R6, sl, :]),
              r=[("tabc", sl)], w=["ropeC"], dma_key=("rc", sl))
            A("sp", lambda e, n=n, sl=sl: e.dma_start(out=ropeS.ap()[:, n * 512:(n + 1) * 512], in_=tabs[R6, sl, :]),
              r=[("tabs", sl)], w=["ropeS"], dma_key=("rs", sl))
        P.barrier()

        def chk(l, ph):
            if l * 10 + ph > maxstage:
                raise _Stop()

        try:
          for l in range(NL):
            chk(l, 0)
            chk(l, 1)
            ar.reset()
            winA = ar.alloc([8, 386], BF16)
            aT = ar.alloc([2, 8, 512], BF16)
            xq = ar.alloc([2, 516], F32)
            xk = ar.alloc([2, 516], F32)
            cvq = ar.alloc([512], F32)
            cvk = ar.alloc([512], F32)
            eq = ar.alloc([2, 512], F32)
            Qml = ar.alloc([S], BF16)
            Kml = ar.alloc([S], BF16)
            vaug = ar.alloc([64, 129], BF16)
            osig = ar.alloc([64, 128], BF16)
            eo = ar.alloc([4, 128], F32)
            graw = ar.alloc([64, 2], F32)
            igt = ar.alloc([64], F32)
            logf = ar.alloc([64], F32)
            bboth = ar.alloc([128], F32)
            bsb = bboth[:, 0:64]
            bend = bboth[:, 64:128]
            gcol = ar.alloc([64], F32)
            dcol = ar.alloc([64], F32)
            wj = ar.alloc([64], F32)
            ebt = ar.alloc([64], F32)
            tmp64 = ar.alloc([64], F32)
            UL = ar.alloc([2, 128], F32)
            DwT = ar.alloc([3, 128], F32)
            sT = ar.alloc([3, 128], BF16)
            H1sb = ar.alloc([2, 129], F32)
            hn = ar.alloc([2, 129], F32)
            sm = ar.alloc([2, 8], F32)
            tg = ar.alloc([2, 128], F32)
            hout = ar.alloc([2, 128], BF16)
            kw = ar.alloc([2, 64], BF16)
            Cst = ar.alloc([129], F32)
            Cprev = ar.alloc([CR, 129], BF16)
            hTst = ar.alloc([2, 512], BF16)
            junk2 = ar.alloc([128], F32)

            winAv = winA_d[l].rearrange("(c p) n -> p c n", p=128)
            A("pool", lambda e, winAv=winAv: e.dma_start(out=winA[:], in_=winAv), w=["winA"], dma_key="winA")
            for c in range(8):
                A("dve", lambda e, c=c, l=l: e.tensor_scalar(out=winA[:, c, :], in0=winA[:, c, :], scalar1=pp[:, l, c:c + 1], scalar2=None, op0=ALU.mult),
                  r=["winA", "pp"], w=["winA"])
            A("dve", lambda e: e.memset(vaug[:, :, 128:129], 1.0), w=["vaug1"])
            A("dve", lambda e: e.memset(xq[0:64, 1, 512:516], 0.0), w=[("xq", 1)])
            A("dve", lambda e: e.memset(xk[0:64, 1, 512:516], 0.0), w=[("xk", 1)])
            A("dve", lambda e: e.memset(Cst[0:64, :], 0.0), w=["Cst"])
            A("dve", lambda e: e.memset(Cprev[0:64, 0, :], 0.0), w=[("Cprev", 0)])
            cw = lambda col, l=l: pp[0:64, l, col:col + 1]
            for n in range(16):
                sl = n % 2
                q = n // 4
                A("sp", lambda e, n=n, sl=sl, q=q: e.dma_start(
                    out=aT[:, sl], in_=agout.ap()[n % 4, q * D:(q + 1) * D, :].rearrange("(c p) t -> p c t", p=128)),
                  r=[("agout", n % 4)], w=[("aT", sl)], dma_key=("aT", sl))
                gbank = {}
                for (col0, nm) in ((0, "xq"), (64, "xk")):
                    b = nb()
                    gbank[nm] = b
                    for c in range(8):
                        A("pe", lambda e, b=b, c=c, sl=sl, col0=col0: e.matmul(ps[0:64, b, :], lhsT=winA[:, c, col0:col0 + 64], rhs=aT[:, sl, c, :],
                                                                            start=(c == 0), stop=(c == 7)),
                          r=["winA", ("aT", sl)], w=[PS(b)])
                tmb = []
                for bl in range(4):
                    b = nb()
                    tmb.append(b)
                    for c in range(8):
                        A("pe", lambda e, b=b, c=c, sl=sl, bl=bl: e.matmul(ps[:, b, 0:258], lhsT=aT[:, sl, c, bl * 128:(bl + 1) * 128], rhs=winA[:, c, 128:386],
                                                                          start=(c == 0), stop=(c == 7)),
                          r=["winA", ("aT", sl)], w=[PS(b)])
                for (xb, nm) in ((xq, "xq"), (xk, "xk")):
                    b = gbank[nm]
                    A("act", lambda e, b=b, xb=xb, sl=sl: e.activation(out=xb[0:64, sl, 3:515], in_=ps[0:64, b, :], func=AF.Copy, scale=1.0),
                      r=[PS(b)], w=[(nm, sl, "m")])
                for (xb, cv, pc, nm) in ((xq, cvq, 21, "q"), (xk, cvk, 26, "k")):
                    A("dve", lambda e, xb=xb, sl=sl: e.tensor_copy(out=xb[0:64, sl, 0:3], in_=xb[0:64, 1 - sl, 512:515]),
                      r=[("x" + nm, 1 - sl, "m"), ("x" + nm, 1 - sl)], w=[("x" + nm, sl, "h")])
                    rr = [("x" + nm, sl, "m"), ("x" + nm, sl, "h"), "pp"]
                    A("dve", lambda e, xb=xb, cv=cv, sl=sl, s1=cw(pc + 3), s2_=cw(pc + 4): e.tensor_scalar(out=cv[0:64, :], in0=xb[0:64, sl, 3:515], scalar1=s1, scalar2=s2_,
                                                                                                          op0=ALU.mult, op1=ALU.add),
                      r=rr, w=[("cv", nm)])
                    for k in range(3):
                        A("dve", lambda e, xb=xb, cv=cv, sl=sl, k=k, s1=cw(pc + k): e.scalar_tensor_tensor(out=cv[0:64, :], in0=xb[0:64, sl, k:k + 512], scalar=s1,
                                                                                                          in1=cv[0:64, :], op0=ALU.mult, op1=ALU.add),
                          r=rr + [("cv", nm)], w=[("cv", nm)])
                    ei = 0 if nm == "q" else 1
                    A("act", lambda e, cv=cv, ei=ei: e.activation(out=eq[0:64, ei, :], in_=cv[0:64, :], func=AF.Exp, scale=-1.0),
                      r=[("cv", nm)], w=[("eq", ei)])
                for bl in range(4):
                    kb = 4 * n + bl
                    b = tmb[bl]
                    es = kb % 4
                    A("act", lambda e, b=b, kb=kb: e.activation(out=vaug[:, kb, 0:128], in_=ps[:, b, 0:128], func=AF.Copy, scale=1.0),
                      r=[PS(b)], w=[("vaug", kb)])
                    A("act", lambda e, b=b, es=es: e.activation(out=eo[:, es, :], in_=ps[:, b, 128:256], func=AF.Exp, scale=-1.0),
                      r=[PS(b)], w=[("eo", es)])
                    A("act", lambda e, b=b, kb=kb: e.activation(out=graw[:, kb, :], in_=ps[:, b, 256:258], func=AF.Copy, scale=1.0), r=[PS(b)], w=[("graw", kb)])
                for (cv, dst, nm) in ((cvq, Qml, "q"), (cvk, Kml, "k")):
                    ei = 0 if nm == "q" else 1
                    A("dve", lambda e, ei=ei: e.tensor_scalar(out=eq[0:64, ei, :], in0=eq[0:64, ei, :], scalar1=1.0, scalar2=None, op0=ALU.add),
                      r=[("eq", ei)], w=[("eq", ei)])
                    A("dve", lambda e, ei=ei: e.reciprocal(out=eq[0:64, ei, :], in_=eq[0:64, ei, :]), r=[("eq", ei)], w=[("eq", ei)])
                    A("dve", lambda e, cv=cv, ei=ei, dst=dst, n=n: e.tensor_tensor(out=dst[0:64, n * 512:(n + 1) * 512], in0=cv[0:64, :], in1=eq[0:64, ei, :], op=ALU.mult),
                      r=[("cv", nm), ("eq", ei)], w=[(nm + "ml", n)])
                for bl in range(4):
                    kb = 4 * n + bl
                    es = kb % 4
                    A("dve", lambda e, es=es: e.tensor_scalar(out=eo[:, es, :], in0=eo[:, es, :], scalar1=1.0, scalar2=None, op0=ALU.add),
                      r=[("eo", es)], w=[("eo", es)])
                    A("dve", lambda e, es=es, kb=kb: e.reciprocal(out=osig[:, kb, :], in_=eo[:, es, :]), r=[("eo", es)], w=[("osig", kb)])
            chk(l, 2)
            GR = [("graw", kb) for kb in range(64)]
            if gcut >= 1:
                A("dve", lambda e, l=l: e.tensor_scalar(out=igt[:], in0=graw[:, :, 0], scalar1=pp[:, l, 19:20], scalar2=None, op0=ALU.add),
                  r=GR + ["pp"], w=["igt"])
            if gcut >= 2:
                A("act", lambda e, l=l: e.activation(out=tmp64[:], in_=graw[:, :, 1], func=AF.Exp, bias=nbf[:, l:l + 1], scale=-1.0),
                  r=GR + ["nbf"], w=["tmp64"])
            if gcut >= 3:
                A("act", lambda e: e.activation(out=tmp64[:], in_=tmp64[:], func=AF.Ln, bias=1.0, scale=1.0), r=["tmp64"], w=["tmp64"])
            if gcut >= 4:
                A("dve", lambda e: e.tensor_scalar(out=logf[:], in0=tmp64[:], scalar1=-1.0, scalar2=None, op0=ALU.mult), r=["tmp64"], w=["logf"])
            bg = nb()
            if gcut >= 5:
                A("pe", lambda e, bg=bg: e.matmul(ps[:, bg, 0:64], lhsT=Um[:], rhs=logf[:], start=True, stop=True), r=["logf", ("c", 3)], w=[PS(bg)])
            if gcut >= 6:
                A("pe", lambda e, bg=bg: e.matmul(ps[:, bg, 64:128], lhsT=onesf[:], rhs=logf[:], start=True, stop=True), r=["logf", ("c", 4)], w=[PS(bg)])
            if gcut >= 7:
                A("dve", lambda e, bg=bg: e.tensor_copy(out=bboth[:], in_=ps[:, bg, 0:128]), r=[PS(bg)], w=["bsb"])
            if gcut >= 8:
                A("dve", lambda e: e.scalar_tensor_tensor(out=gcol[:], in0=igt[:], scalar=LN8, in1=bsb, op0=ALU.add, op1=ALU.subtract),
                  r=["igt", "bsb"], w=["gcol"])
            if gcut >= 9:
                A("act", lambda e: e.activation(out=dcol[:], in_=bend, func=AF.Exp, scale=1.0), r=["bsb"], w=["dcol"])
            if gcut >= 10:
                A("dve", lambda e: e.tensor_tensor(out=tmp64[:], in0=bend, in1=gcol[:], op=ALU.add), r=["gcol", "bsb"], w=["tmp64"])
            if gcut >= 11:
                A("act", lambda e: e.activation(out=wj[:], in_=tmp64[:], func=AF.Exp, scale=1.0), r=["tmp64"], w=["wj"])
            if gcut >= 12:
                A("act", lambda e: e.activation(out=ebt[:], in_=bsb, func=AF.Exp, scale=1.0), r=["bsb"], w=["ebt"])

            hsv = hsend.ap()
            def cbank(c):
                return c % 2

            BCL, BDT, BH1, BHT = 2, 3, 4, 7

            def A1(c):
                s2 = c % 2
                s3 = c % 3
                cs = slice(c * 128, (c + 1) * 128)
                n = c // 4
                cb = cbank(c)
                A("pe", lambda e: e.matmul(ps[:, cb, 0:64], lhsT=Kml[0:64, cs], rhs=identb[0:64, 0:64], start=True, stop=True),
                  r=[("kml", n), ("c", 0)], w=[PS(cb)])
                A("dve", lambda e: e.tensor_scalar(out=UL[:, s2, :], in0=Um[:], scalar1=logf[:, c:c + 1], scalar2=None, op0=ALU.mult),
                  r=["logf", ("c", 3)], w=[("UL", s2)])
                A("pe", lambda e: e.matmul(ps[:, BDT, 0:128], lhsT=onesf[:], rhs=UL[:, s2, :], start=True, stop=False),
                  r=[("UL", s2), ("c", 4)], w=[PS(BDT)])
                A("pe", lambda e: e.matmul(ps[:, BDT, 0:128], lhsT=identb[:], rhs=negmask[:], start=False, stop=True),
                  r=[("c", 0), ("c", 1)], w=[PS(BDT)])
                A("act", lambda e: e.activation(out=DwT[:, s3, :], in_=ps[:, BDT, 0:128], func=AF.Exp, bias=gcol[:, c:c + 1], scale=1.0),
                  r=[PS(BDT), "gcol"], w=[("DwT", s3)])
                A("pe", lambda e: e.matmul(ps[:, cb, 64:192], lhsT=Kml[0:64, cs], rhs=Qml[0:64, cs], start=True, stop=True),
                  r=[("qml", n), ("kml", n)], w=[PS(cb)])

            def A2(c):
                s2 = c % 2
                s3 = c % 3
                cb = cbank(c)
                A("dve", lambda e: e.tensor_scalar(out=kw[:, s2, :], in0=ps[:, cb, 0:64], scalar1=wj[:, c:c + 1], scalar2=None, op0=ALU.mult),
                  r=[PS(cb), "wj"], w=[("kw", s2)])
                A("dve", lambda e: e.tensor_tensor(out=sT[:, s3, :], in0=ps[:, cb, 64:192], in1=DwT[:, s3, :], op=ALU.mult),
                  r=[PS(cb), ("DwT", s3)], w=[("sT", s3)])
                A("pe", lambda e: e.matmul(ps[0:64, BCL, 0:129], lhsT=kw[:, s2, :], rhs=vaug[:, c, :], start=True, stop=True),
                  r=[("kw", s2), ("vaug", c), "vaug1"], w=[PS(BCL)])

            def A3(c):
                cb = cbank(c)
                A("dve", lambda e: e.tensor_scalar(out=Cst[0:64, :], in0=Cst[0:64, :], scalar1=dcol[0:64, c:c + 1], scalar2=None, op0=ALU.mult),
                  r=["Cst", "dcol"], w=["Cst"])
                A("dve", lambda e: e.tensor_tensor(out=Cst[0:64, :], in0=ps[0:64, BCL, 0:129], in1=Cst[0:64, :], op=ALU.add),
                  r=[PS(BCL), "Cst"], w=["Cst"])
                if c < 63:
                    A("act", lambda e: e.activation(out=Cprev[0:64, (c + 1) % CR, :], in_=Cst[0:64, :], func=AF.Copy, scale=1.0),
                      r=["Cst"], w=[("Cprev", (c + 1) % CR)])

            def B1(c):
                s2 = c % 2
                s3 = c % 3
                cs = slice(c * 128, (c + 1) * 128)
                n = c // 4
                bh2 = 5 + c % 2
                A("pe", lambda e: e.matmul(ps[:, BH1, 0:129], lhsT=sT[:, s3, :], rhs=vaug[:, c, :], start=True, stop=True),
                  r=[("sT", s3), ("vaug", c), "vaug1"], w=[PS(BH1)])
                A("pe", lambda e: e.matmul(ps[:, bh2, 0:129], lhsT=Qml[0:64, cs], rhs=Cprev[0:64, c % CR, :], start=True, stop=True),
                  r=[("qml", n), ("Cprev", c % CR)], w=[PS(bh2)])
                A("act", lambda e: e.activation(out=H1sb[:, s2, :], in_=ps[:, BH1, 0:129], func=AF.Copy, scale=1.0), r=[PS(BH1)], w=[("H1sb", s2)])

            def B2(c):
                s2 = c % 2
                bh2 = 5 + c % 2
                A("dve", lambda e: e.scalar_tensor_tensor(out=hn[:, s2, :], in0=ps[:, bh2, 0:129], scalar=ebt[:, c:c + 1], in1=H1sb[:, s2, :],
                                                          op0=ALU.mult, op1=ALU.add),
                  r=[PS(bh2), ("H1sb", s2), "ebt"], w=[("hn", s2)])
                A("dve", lambda e: e.tensor_scalar(out=sm[:, s2, 0:1], in0=hn[:, s2, 128:129], scalar1=-1.0, scalar2=None, op0=ALU.mult),
                  r=[("hn", s2)], w=[("sm0", s2)])
                A("dve", lambda e: e.scalar_tensor_tensor(out=sm[:, s2, 1:2], in0=hn[:, s2, 128:129], scalar=1.0, in1=sm[:, s2, 0:1], op0=ALU.max, op1=ALU.max),
                  r=[("hn", s2), ("sm0", s2)], w=[("sm1", s2)])
                A("dve", lambda e: e.reciprocal(out=sm[:, s2, 2:3], in_=sm[:, s2, 1:2]), r=[("sm1", s2)], w=[("sm2", s2)])
                A("act", lambda e: e.activation(out=junk2[:], in_=hn[:, s2, 0:128], func=AF.Square, scale=sm[:, s2, 2:3], accum_out=sm[:, s2, 3:4]),
                  r=[("hn", s2), ("sm2", s2)], w=["junk2", ("sm3", s2)])
                A("act", lambda e: e.activation(out=sm[:, s2, 4:5], in_=sm[:, s2, 3:4], func=AF.Ln, bias=EPS, scale=1.0 / 128), r=[("sm3", s2)], w=[("sm4", s2)])
                A("act", lambda e: e.activation(out=sm[:, s2, 5:6], in_=sm[:, s2, 4:5], func=AF.Exp, scale=-0.5), r=[("sm4", s2)], w=[("sm5", s2)])

            def B3(c):
                s2 = c % 2
                A("dve", lambda e: e.tensor_tensor(out=sm[:, s2, 6:7], in0=sm[:, s2, 2:3], in1=sm[:, s2, 5:6], op=ALU.mult),
                  r=[("sm2", s2), ("sm5", s2)], w=[("sm6", s2)])
                A("dve", lambda e, l=l: e.tensor_tensor(out=tg[:, s2, :], in0=osig[:, c, :], in1=gml[:, l, :], op=ALU.mult),
                  r=[("osig", c), "gml"], w=[("tg", s2)])
                A("dve", lambda e: e.scalar_tensor_tensor(out=hout[:, s2, :], in0=hn[:, s2, 0:128], scalar=sm[:, s2, 6:7], in1=tg[:, s2, :], op0=ALU.mult, op1=ALU.mult),
                  r=[("hn", s2), ("sm6", s2), ("tg", s2)], w=[("hout", s2)])
                A("pe", lambda e: e.matmul(ps[:, BHT, 0:128], lhsT=hout[:, s2, :], rhs=identb[:], start=True, stop=True),
                  r=[("hout", s2), ("c", 0)], w=[PS(BHT)])

            def B4(c):
                hs = (c // 4) % 2
                A("act", lambda e: e.activation(out=hTst[:, hs, (c % 4) * 128:(c % 4 + 1) * 128], in_=ps[:, BHT, 0:128], func=AF.Copy, scale=1.0),
                  r=[PS(BHT)], w=[("hTst", hs, c % 4)])
                if c % 4 == 3:
                    A("sp", lambda e: e.dma_start(out=hsv[c // 16, 0:128, ((c // 4) % 4) * 512:((c // 4) % 4 + 1) * 512], in_=hTst[:, hs, :]),
                      r=[("hTst", hs, k) for k in range(4)], w=[("hsend", "ml", c // 4)], dma_key=("hT", hs))

            STG = ((A2, 1), (A1, 0), (B1, 2), (B2, 3), (B4, 5), (B3, 4), (A3, 1))
            NCH = 64 if cut >= 1 else 0
            for it in range(NCH + 5 if NCH else 0):
                for fn_, off_ in STG:
                    c_ = it - off_
                    if 0 <= c_ < NCH:
                        fn_(c_)
            P.barrier()

            chk(l, 3)
            ar.reset()
            winB = ar.alloc([8, 448], BF16)
            wuq = ar.alloc([3, 384], BF16)
            wukv = ar.alloc([256], BF16)
            aTb = ar.alloc([2, 8, 512], BF16)
            C2 = ar.alloc([2, 512], F32)
            S2 = ar.alloc([2, 512], F32)
            cqr = ar.alloc([3, 512], BF16)
            cqs = ar.alloc([3, 512], BF16)
            cqn = ar.alloc([3, 512], BF16)
            rbc = ar.alloc([512], F32)
            ckr = ar.alloc([512], BF16)
            cks = ar.alloc([512], BF16)
            ckn = ar.alloc([512], BF16)
            rbc2 = ar.alloc([512], F32)
            t1 = ar.alloc([512], F32)
            t2 = ar.alloc([512], F32)
            QT = ar.alloc([2, 2, 512], BF16)
            KT = ar.alloc([2, S], BF16)
            Vt = ar.alloc([64, 2, 65], BF16)
            PT = ar.alloc([3, 2, 512], BF16)
            Osb = ar.alloc([2, 512], F32)
            rden = t2
            hst = ar.alloc([2, 512], BF16)

            A("pool", lambda e, l=l: e.dma_start(out=winB[:], in_=winB_d[l].rearrange("(c p) n -> p c n", p=128)), w=["winB"], dma_key="winB")
            for c in range(8):
                A("dve", lambda e, c=c, l=l: e.tensor_scalar(out=winB[:, c, :], in0=winB[:, c, :], scalar1=pp[:, l, c:c + 1], scalar2=None, op0=ALU.mult),
                  r=["winB", "pp"], w=["winB"])
            A("pool", lambda e, l=l: e.dma_start(out=wuq[:], in_=wuq_d[l].rearrange("k p n -> p k n")), w=["wuq"], dma_key="wuq")
            for k in range(3):
                A("dve", lambda e, k=k, l=l: e.tensor_scalar(out=wuq[:, k, :], in0=wuq[:, k, :], scalar1=pp[:, l, 16 + k:17 + k], scalar2=None, op0=ALU.mult),
                  r=["wuq", "pp"], w=["wuq"])
            A("pool", lambda e, l=l: e.dma_start(out=wukv[:], in_=wukv_d[l]), w=["wukv"], dma_key="wukv")
            A("dve", lambda e, l=l: e.tensor_scalar(out=wukv[:], in0=wukv[:], scalar1=pp[:, l, 34:35], scalar2=None, op0=ALU.mult),
              r=["wukv", "pp"], w=["wukv"])
            A("dve", lambda e: e.memset(Vt[:, :, :, 64:65], 1.0), w=["Vt1"])
            KCH = ((64, 0), (64, 1), (128, 2))
            MISC = (4, 5)
            SPAIRS = ((0, 1), (2, 3))
            sp_ctr = [0]
            octr = [0]
            hsvv = hsend.ap()

            def proj_steps(G):
                sl = G % 2
                q = G // 4
                cs = slice(G * 512, (G + 1) * 512)
                st_ = {}

                def s0():
                    A("sp", lambda e: e.dma_start(
                        out=aTb[:, sl], in_=agout.ap()[G % 4, q * D:(q + 1) * D, :].rearrange("(c p) t -> p c t", p=128)),
                      r=[("agout", G % 4)], w=[("aTb", sl)], dma_key=("aTb", sl))
                    A("sp", lambda e: e.dma_start(out=C2[R6, sl, :], in_=ropeC.ap()[:, G * 512:(G + 1) * 512]), r=["ropeC"], w=[("C2", sl)], dma_key=("C2", sl))
                    A("sp", lambda e: e.dma_start(out=S2[R6, sl, :], in_=ropeS.ap()[:, G * 512:(G + 1) * 512]), r=["ropeS"], w=[("S2", sl)], dma_key=("S2", sl))

                def grp(gi, col0, M):
                    bnk = nb(MISC)
                    st_[("gb", gi)] = bnk
                    for c in range(8):
                        A("pe", lambda e, c=c: e.matmul(ps[0:M, bnk, :], lhsT=winB[:, c, col0:col0 + M], rhs=aTb[:, sl, c, :],
                                                       start=(c == 0), stop=(c == 7)),
                          r=["winB", ("aTb", sl)], w=[PS(bnk)])
                    return bnk

                def s1():
                    for gi, col0 in ((0, 0), (1, 96)):
                        bnk = grp(gi, col0, 96)
                        A("dve", lambda e, gi=gi, bnk=bnk: e.tensor_copy(out=cqr[0:64, gi, :], in_=ps[0:64, bnk, :]), r=[PS(bnk)], w=[("cqr", gi)])
                        if gi == 0:
                            A("dve", lambda e, bnk=bnk: e.tensor_tensor(out=t1[R6, :], in0=ps[R6, bnk, :], in1=C2[R6, sl, :], op=ALU.mult), r=[PS(bnk), ("C2", sl)], w=["t1"])
                        else:
                            A("dve", lambda e, bnk=bnk: e.tensor_tensor(out=t2[R6, :], in0=ps[R6, bnk, :], in1=S2[R6, sl, :], op=ALU.mult), r=[PS(bnk), ("S2", sl)], w=["t2"])
                        A("act", lambda e, gi=gi: e.activation(out=cqs[0:64, gi, :], in_=cqr[0:64, gi, :], func=AF.Square, scale=1.0), r=[("cqr", gi)], w=[("cqs", gi)])
                    A("dve", lambda e: e.tensor_tensor(out=KT[R6, 0, cs], in0=t1[R6, :], in1=t2[R6, :], op=ALU.add), r=["t1", "t2"], w=[("KTr", 0, G)])
                    A("dve", lambda e: e.tensor_copy(out=KT[R6, 1, cs], in_=KT[R6, 0, cs]), r=[("KTr", 0, G)], w=[("KTr", 1, G)])

                def s2():
                    bnk = grp(2, 192, 128)
                    A("dve", lambda e: e.tensor_copy(out=cqr[:, 2, :], in_=ps[:, bnk, :]), r=[PS(bnk)], w=[("cqr", 2)])
                    A("act", lambda e: e.activation(out=cqs[:, 2, :], in_=cqr[:, 2, :], func=AF.Square, scale=1.0), r=[("cqr", 2)], w=[("cqs", 2)])
                    bnk2 = grp(3, 320, 128)
                    A("dve", lambda e: e.tensor_copy(out=ckr[:], in_=ps[:, bnk2, :]), r=[PS(bnk2)], w=["ckr"])
                    A("act", lambda e: e.activation(out=cks[:], in_=ckr[:], func=AF.Square, scale=1.0), r=["ckr"], w=["cks"])

                def s3():
                    bq = nb(MISC)
                    for k, (rows, _) in enumerate(KCH):
                        A("pe", lambda e, k=k, rows=rows: e.matmul(ps[:, bq, :], lhsT=onesb[0:rows, :], rhs=cqs[0:rows, k, :], start=(k == 0), stop=(k == 2)),
                          r=[("cqs", k), ("c", 2)], w=[PS(bq)])
                    A("act", lambda e: e.activation(out=rbc[:], in_=ps[:, bq, :], func=AF.Ln, bias=EPS, scale=1.0 / 256), r=[PS(bq)], w=["rbc"])
                    A("act", lambda e: e.activation(out=rbc[:], in_=rbc[:], func=AF.Exp, scale=-0.5), r=["rbc"], w=["rbc"])
                    bk = nb(MISC)
                    A("pe", lambda e: e.matmul(ps[:, bk, :], lhsT=onesb[:], rhs=cks[:], start=True, stop=True), r=["cks", ("c", 2)], w=[PS(bk)])
                    A("act", lambda e: e.activation(out=rbc2[:], in_=ps[:, bk, :], func=AF.Ln, bias=EPS, scale=1.0 / 128), r=[PS(bk)], w=["rbc2"])
                    A("act", lambda e: e.activation(out=rbc2[:], in_=rbc2[:], func=AF.Exp, scale=-0.5), r=["rbc2"], w=["rbc2"])

                def s4():
                    for k, (rows, _) in enumerate(KCH):
                        A("dve", lambda e, k=k, rows=rows: e.tensor_tensor(out=cqn[0:rows, k, :], in0=cqr[0:rows, k, :], in1=rbc[0:rows, :], op=ALU.mult),
                          r=[("cqr", k), "rbc"], w=[("cqn", k)])
                    A("dve", lambda e: e.tensor_tensor(out=ckn[:], in0=ckr[:], in1=rbc2[:], op=ALU.mult), r=["ckr", "rbc2"], w=["ckn"])

                def qhead(hh):
                    br = nb(MISC)
                    for k, (rows, _) in enumerate(KCH):
                        A("pe", lambda e, k=k, rows=rows: e.matmul(ps[0:96, br, :], lhsT=wuq[0:rows, k, hh * 192:hh * 192 + 96], rhs=cqn[0:rows, k, :],
                                                                  start=(k == 0), stop=(k == 2)),
                          r=["wuq", ("cqn", k)], w=[PS(br)])
                    A("dve", lambda e: e.tensor_copy(out=QT[0:64, sl, hh, :], in_=ps[0:64, br, :]), r=[PS(br)], w=[("QTn", sl, hh)])
                    A("dve", lambda e: e.tensor_tensor(out=t1[R6, :], in0=ps[R6, br, :], in1=C2[R6, sl, :], op=ALU.mult), r=[PS(br), ("C2", sl)], w=["t1"])
                    bs = nb(MISC)
                    for k, (rows, _) in enumerate(KCH):
                        A("pe", lambda e, k=k, rows=rows: e.matmul(ps[0:96, bs, :], lhsT=wuq[0:rows, k, hh * 192 + 96:hh * 192 + 192], rhs=cqn[0:rows, k, :],
                                                                  start=(k == 0), stop=(k == 2)),
                          r=["wuq", ("cqn", k)], w=[PS(bs)])
                    A("dve", lambda e: e.tensor_tensor(out=t2[R6, :], in0=ps[R6, bs, :], in1=S2[R6, sl, :], op=ALU.mult), r=[PS(bs), ("S2", sl)], w=["t2"])
                    A("dve", lambda e: e.tensor_tensor(out=QT[R6, sl, hh, :], in0=t1[R6, :], in1=t2[R6, :], op=ALU.add), r=["t1", "t2"], w=[("QTr", sl, hh)])

                def s5():
                    qhead(0)

                def s6():
                    qhead(1)

                def s7():
                    for hh in range(2):
                        bnk = nb(MISC)
                        A("pe", lambda e, hh=hh, bnk=bnk: e.matmul(ps[0:64, bnk, :], lhsT=wukv[:, hh * 64:(hh + 1) * 64], rhs=ckn[:], start=True, stop=True),
                          r=["wukv", "ckn"], w=[PS(bnk)])
                        A("act", lambda e, hh=hh, bnk=bnk: e.activation(out=KT[0:64, hh, cs], in_=ps[0:64, bnk, :], func=AF.Copy, scale=1.0), r=[PS(bnk)], w=[("KTn", hh, G)])
                    bv = nb(MISC)
                    for bl in range(4):
                        A("pe", lambda e, bl=bl: e.matmul(ps[:, bv, bl * 128:(bl + 1) * 128], lhsT=ckn[:, bl * 128:(bl + 1) * 128], rhs=wukv[:, 128:256], start=True, stop=True),
                          r=["wukv", "ckn"], w=[PS(bv)])
                    A("dve", lambda e: e.tensor_copy(out=Vt[:, 4 * G:4 * G + 4, :, 0:64], in_=ps[:, bv, :].rearrange("p (b h d) -> p b h d", b=4, h=2)),
                      r=[PS(bv)], w=[("Vt", G)])

                return [s0, s1, s2, s3, s4, s5, s6, s7]

            pend = []

            def attention(G, hh, extra):
                sl = G % 2
                ob = (6, 7)[octr[0] % 2]
                octr[0] += 1
                Qr = [("QTn", sl, hh), ("QTr", sl, hh)]
                nkb = 4 * G + 4
                units = [(kb, kb + 1) for kb in range(0, 4 * G, 2)] + [(kb,) for kb in range(4 * G, nkb)]
                info = {}

                def qk(ui):
                    u = units[ui]
                    b0, b1 = SPAIRS[sp_ctr[0] % 2]
                    sp_ctr[0] += 1
                    res = ("ps2", b0)
                    c0 = 0
                    for j, kb in enumerate(u):
                        bnk = (b0, b1)[j]
                        kg = kb // 4
                        Kr = [("KTn", hh, kg), ("KTr", hh, kg)]
                        ks = slice(kb * 128, (kb + 1) * 128)
                        a_ = kb - 4 * G
                        if a_ < 0:
                            A("pe", lambda e, bnk=bnk, ks=ks: e.matmul(ps[:, bnk, :], lhsT=KT[0:96, hh, ks], rhs=QT[0:96, sl, hh, :], start=True, stop=True),
                              r=Kr + Qr, w=[res])
                        else:
                            c0 = a_ * 128
                            A("pe", lambda e, bnk=bnk, ks=ks, c0=c0: e.matmul(ps[:, bnk, c0:c0 + 128], lhsT=KT[0:96, hh, ks], rhs=QT[0:96, sl, hh, c0:c0 + 128],
                                                                           start=True, stop=False),
                              r=Kr + Qr, w=[res])
                            A("pe", lambda e, bnk=bnk, c0=c0: e.matmul(ps[:, bnk, c0:c0 + 128], lhsT=identb[:], rhs=negmask[:], start=False, stop=True),
                              r=[("c", 0), ("c", 1)], w=[res])
                            if a_ < 3:
                                A("pe", lambda e, bnk=bnk, ks=ks, c0=c0: e.matmul(ps[:, bnk, c0 + 128:512], lhsT=KT[0:96, hh, ks], rhs=QT[0:96, sl, hh, c0 + 128:512],
                                                                               start=True, stop=True),
                                  r=Kr + Qr, w=[res])
                    info[ui] = (b0, res, c0)

                def pv(ui):
                    u = units[ui]
                    b0, res, c0 = info[ui]
                    p3 = ui % 3
                    if len(u) == 2:
                        A("act", lambda e: e.activation(out=PT[:, p3, :, :], in_=ps[:, b0:b0 + 2, :], func=AF.Exp, scale=QSCALE),
                          r=[res], w=[("PT", p3)])
                    else:
                        A("act", lambda e: e.activation(out=PT[:, p3, 0, c0:512], in_=ps[:, b0, c0:512], func=AF.Exp, scale=QSCALE),
                          r=[res], w=[("PT", p3)])
                    for j, kb in enumerate(u):
                        kg = kb // 4
                        A("pe", lambda e, j=j, kb=kb: e.matmul(ps[0:65, ob, c0:512], lhsT=Vt[:, kb, hh, :], rhs=PT[:, p3, j, c0:512],
                                                               start=(kb == 0), stop=(kb == nkb - 1)),
                          r=[("PT", p3), ("Vt", kg), "Vt1"], w=[PS(ob)])

                nu = len(units)
                for it in range(nu + 1):
                    if it < nu:
                        qk(it)
                    if it >= 1:
                        pv(it - 1)
                    if it == min(2, nu) and pend:
                        pend.pop(0)()
                    if extra and ((it == 0 and hh == 0) or (it >= 4 and it % 4 == 0)):
                        extra.pop(0)()
                A("act", lambda e: e.activation(out=Osb[0:65, hh, :], in_=ps[0:65, ob, :], func=AF.Copy, scale=1.0), r=[PS(ob)], w=[("Osb", hh)])
                A("dve", lambda e: e.reciprocal(out=Osb[64:65, hh, :], in_=Osb[64:65, hh, :]), r=[("Osb", hh)], w=[("Osb", hh)])

                def epilogue():
                    bb = nb(MISC)
                    A("pe", lambda e: e.matmul(ps[0:64, bb, :], lhsT=onesf[64:65, 0:64], rhs=Osb[64:65, hh, :], start=True, stop=True), r=[("Osb", hh), ("c", 4)], w=[PS(bb)])
                    A("dve", lambda e: e.tensor_tensor(out=hst[0:64, hh, :], in0=ps[0:64, bb, :], in1=Osb[0:64, hh, :], op=ALU.mult),
                      r=[("Osb", hh), PS(bb)], w=[("hst", hh)])
                    A("sp", lambda e: e.dma_start(out=hsvv[G // 4, 128 + 64 * hh:192 + 64 * hh, (G % 4) * 512:(G % 4 + 1) * 512], in_=hst[0:64, hh, :]),
                      r=[("hst", hh)], w=[("hsend", "mla", hh, G)], dma_key=("hst", hh))

                pend.append(epilogue)

            for stp in proj_steps(0):
                stp()
            for G in range(16):
                extra = proj_steps(G + 1) if G + 1 < 16 else []
                for hh in range(2):
                    attention(G, hh, extra)
                while extra:
                    extra.pop(0)()
                if G % 4 == 3 or G == 15:
                    while pend:
                        pend.pop(0)()
                if G % 4 == 3 and not sim:
                    kq = G // 4
                    A("pool", lambda e, kq=kq: e.collective_compute("AllGather", ALU.bypass, replica_groups=GROUPS,
                                                                    ins=[hsend.ap()[kq].opt()], outs=[hall.ap()[kq].opt()]),
                      r=[("hsend", "ml", n_) for n_ in range(4 * kq, 4 * kq + 4)] + [("hsend", "mla", h_, g_) for h_ in range(2) for g_ in range(4 * kq, 4 * kq + 4)],
                      w=[("hall", kq)], dma_key=("cc2", kq), inc=1)
            P.barrier()

            chk(l, 4)
            ar.reset()
            hcat = ar.alloc([8, TOK], BF16)
            wo = ar.alloc([8, D], BF16)
            gpm = ar.alloc([D], F32)
            tt = ar.alloc([2, D], F32)
            st4 = ar.alloc([NBLK, 4], F32)
            junk4 = ar.alloc([512], BF16)
            A("pool", lambda e, l=l: e.dma_start(out=wo[:], in_=wout_d[l].rearrange("(c p) n -> p c n", p=128)), w=["wo"], dma_key="wo")
            A("sp", lambda e, l=l: e.dma_start(out=gpm[:], in_=gpost_d[l, 0]), w=["gpm"], dma_key="gpm")
            HALL = [("hall", kq) for kq in range(4)]
            if l == 0:
                dbg_dump("d_hall", hall.ap().rearrange("q r t -> (q r) t"), HALL)
            hallv = hall.ap().rearrange("q r t -> (q r) t")
            for k in range(8):
                A("pool", lambda e, k=k: e.indirect_dma_start(out=hcat[:, k, :], out_offset=None, in_=hallv,
                                                              in_offset=bass.IndirectOffsetOnAxis(ap=gidx[:, k:k + 1], axis=0)),
                  r=HALL + [("c", 5)], w=[("hcat", k)], dma_key=("hcat", k))

            def post_norm(i, srcs, gtile, l=l, final=False, st4=st4, junk4=junk4, tt=tt, frompsum=False):
                s2 = i % 2
                for half, (src, res) in enumerate(srcs):
                    A("act", lambda e, src=src, half=half: e.activation(out=junk4[:], in_=src, func=AF.Square, accum_out=st4[:, i, half:half + 1], scale=1.0),
                      r=[res], w=["junk4", ("st4", i, half)])
                A("dve", lambda e: e.tensor_tensor(out=st4[:, i, 2:3], in0=st4[:, i, 0:1], in1=st4[:, i, 1:2], op=ALU.add),
                  r=[("st4", i, 0), ("st4", i, 1)], w=[("st4", i, 2)])
                A("act", lambda e: e.activation(out=st4[:, i, 3:4], in_=st4[:, i, 2:3], func=AF.Ln, bias=EPS, scale=1.0 / D), r=[("st4", i, 2)], w=[("st4", i, 3)])
                A("act", lambda e: e.activation(out=st4[:, i, 3:4], in_=st4[:, i, 3:4], func=AF.Exp, scale=-0.5), r=[("st4", i, 3)], w=[("st4", i, 3)])
                for half, (src, res) in enumerate(srcs):
                    if frompsum:
                        A("act", lambda e, src=src, half=half: e.activation(out=tt[:, s2, half * 512:(half + 1) * 512], in_=src, func=AF.Copy, scale=1.0),
                          r=[res], w=[("tt", s2, half)])
                        A("dve", lambda e, half=half: e.tensor_tensor(out=tt[:, s2, half * 512:(half + 1) * 512], in0=tt[:, s2, half * 512:(half + 1) * 512],
                                                                       in1=gtile[:, half * 512:(half + 1) * 512], op=ALU.mult),
                          r=[("tt", s2, half), "gtile"], w=[("tt", s2, half)])
                    else:
                        A("dve", lambda e, src=src, half=half: e.tensor_tensor(out=tt[:, s2, half * 512:(half + 1) * 512], in0=src, in1=gtile[:, half * 512:(half + 1) * 512], op=ALU.mult),
                          r=[res, "gtile"], w=[("tt", s2, half)])
                A("dve", lambda e: e.scalar_tensor_tensor(out=X[:, i, :], in0=tt[:, s2, :], scalar=st4[:, i, 3:4], in1=X[:, i, :], op0=ALU.mult, op1=ALU.add),
                  r=[("tt", s2, 0), ("tt", s2, 1), ("st4", i, 3), ("X", i)], w=[("X", i)])

            A("dve", lambda e: e.tensor_copy(out=gpm[:, 0:1], in_=gpm[:, 0:1]), r=["gpm"], w=["gtile"])
            for i in range(NBLK):
                bh = [nb(), nb()]
                for half in range(2):
                    for k in range(8):
                        A("pe", lambda e, b=bh[half], k=k, i=i, half=half: e.matmul(ps[:, b, :], lhsT=hcat[:, k, i * 128:(i + 1) * 128], rhs=wo[:, k, half * 512:(half + 1) * 512],
                                                                                 start=(k == 0), stop=(k == 7)),
                          r=[("hcat", k), "wo"], w=[PS(bh[half])])
                post_norm(i, [(ps[:, bh[0], :], PS(bh[0])), (ps[:, bh[1], :], PS(bh[1]))], gpm, frompsum=True)
            if l == 0:
                dbg_dump("d_xmix", X[:].rearrange("p i d -> p (i d)"), [("X", i) for i in range(NBLK)])
            P.barrier()

            chk(l, 5)
            ar.reset()
            mT = ar.alloc([8, 1024], BF16)
            ysb = ar.alloc([8, D], F32)
            hT = ar.alloc([2, 4, 1024], BF16)
            wu = ar.alloc([2, 8, 512], BF16)
            wd = ar.alloc([2, 4, D], BF16)
            h1 = ar.alloc([3, 512], BF16)
            gpl = ar.alloc([D], F32)
            tt = ar.alloc([2, D], F32)
            st4 = ar.alloc([NBLK, 4], F32)
            junk4 = ar.alloc([512], BF16)
            A("sp", lambda e, l=l: e.dma_start(out=gpl[:], in_=gpost_d[l, 1]), w=["gpl"], dma_key="gpl")
            A("dve", lambda e: e.tensor_copy(out=gpl[:, 0:1], in_=gpl[:, 0:1]), r=["gpl"], w=["gtile"])
            outv = out_d.rearrange("(i p) d -> p i d", p=128)
            h1c = 0
            for tgp in range(2):
                blocks = range(8 * tgp, 8 * tgp + 8)
                norm_T(blocks, lambda ii, half: (mT[:, half * 4:(half + 1) * 4, ii * 128:(ii + 1) * 128], ("mT", ii, half)))
                MT = [("mT", ii, h) for ii in range(8) for h in range(2)]
                for s in range(8):
                    ws = s % 2
                    A("pool", lambda e, s=s, ws=ws, l=l: e.dma_start(out=wu[:, ws], in_=wup_d[l][:, s * 512:(s + 1) * 512].rearrange("(c p) f -> p c f", p=128)),
                      w=[("wu", ws)], dma_key=("wu", ws))
                    for c in range(8):
                        A("dve", lambda e, c=c, ws=ws, l=l: e.tensor_scalar(out=wu[:, ws, c, :], in0=wu[:, ws, c, :], scalar1=pp[:, l, 8 + c:9 + c], scalar2=None, op0=ALU.mult),
                          r=[("wu", ws), "pp"], w=[("wu", ws)])
                    A("pool", lambda e, s=s, ws=ws, l=l: e.dma_start(out=wd[:, ws], in_=wdn_d[l][s * 512:(s + 1) * 512, :].rearrange("(c p) n -> p c n", p=128)),
                      w=[("wd", ws)], dma_key=("wd", ws))
                    for fc in range(4):
                        for th in range(2):
                            b = nb()
                            for c in range(8):
                                A("pe", lambda e, b=b, c=c, ws=ws, fc=fc, th=th: e.matmul(ps[:, b, :], lhsT=wu[:, ws, c, fc * 128:(fc + 1) * 128], rhs=mT[:, c, th * 512:(th + 1) * 512],
                                                                                      start=(c == 0), stop=(c == 7)),
                                  r=[("wu", ws)] + MT, w=[PS(b)])
                            hs_ = h1c % 3
                            h1c += 1
                            A("act", lambda e, b=b, hs_=hs_: e.activation(out=h1[:, hs_, :], in_=ps[:, b, :], func=AF.Relu, scale=1.0), r=[PS(b)], w=[("h1", hs_)])
                            A("dve", lambda e, hs_=hs_, ws=ws, fc=fc, th=th: e.tensor_tensor(out=hT[:, ws, fc, th * 512:(th + 1) * 512], in0=h1[:, hs_, :], in1=h1[:, hs_, :], op=ALU.mult),
                              r=[("h1", hs_)], w=[("hT", ws, fc, th)])
                    for bl in range(8):
                        for half in range(2):
                            b = nb()
                            for fc in range(4):
                                A("pe", lambda e, b=b, fc=fc, ws=ws, bl=bl, half=half: e.matmul(ps[:, b, :], lhsT=hT[:, ws, fc, bl * 128:(bl + 1) * 128], rhs=wd[:, ws, fc, half * 512:(half + 1) * 512],
                                                                                             start=(fc == 0), stop=(fc == 3)),
                                  r=[("hT", ws, fc, bl // 4), ("wd", ws)], w=[PS(b)])
                            ydst = ysb[:, bl, half * 512:(half + 1) * 512]
                            if s == 0:
                                A("act", lambda e, b=b, ydst=ydst: e.activation(out=ydst, in_=ps[:, b, :], func=AF.Copy, scale=1.0), r=[PS(b)], w=[("ysb", bl, half)])
                            else:
                                A("dve", lambda e, b=b, ydst=ydst: e.tensor_tensor(out=ydst, in0=ps[:, b, :], in1=ydst, op=ALU.add), r=[PS(b), ("ysb", bl, half)], w=[("ysb", bl, half)])
                for bl in range(8):
                    i = 8 * tgp + bl
                    post_norm(i, [(ysb[:, bl, 0:512], ("ysb", bl, 0)), (ysb[:, bl, 512:1024], ("ysb", bl, 1))], gpl, st4=st4, junk4=junk4, tt=tt)
                    if l == NL - 1:
                        A("sp", lambda e, i=i: e.dma_start(out=outv[:, i, :], in_=X[:, i, :]), r=[("X", i)], dma_key=("out", i % 4))
                if l < NL - 1 and l * 10 + 10 <= maxstage:
                    emit_P0(l + 1, range(2 * tgp, 2 * tgp + 2))
            P.barrier()

        except _Stop:
            P.barrier()
            outv = out_d.rearrange("(i p) d -> p i d", p=128)
            for i in range(NBLK):
                A("sp", lambda e, i=i: e.dma_start(out=outv[:, i, :], in_=X[:, i, :]), r=[("X", i)], dma_key=("out", i % 4))

        with nc.allow_low_precision(reason="bf16 matmul operands by design; all accumulation/statistics in fp32"):
            P.emit(st)
    return nc


def _prep_inputs(inp):
    f32 = np.float32
    bf = ml_dtypes.bfloat16
    x = np.asarray(inp["x"], f32)
    pos = np.asarray(inp["positions"]).astype(np.int32)
    w_in = np.asarray(inp["w_in"], f32)
    o0, o1, o2, o3, o4, o5 = 512, 1024, 1536, 1544, 1800, 1928
    jj = np.arange(128)
    identb = np.eye(128, dtype=f32).astype(bf)
    negmask = np.where(jj[None, :] >= jj[:, None], 0.0, -30000.0).astype(f32).astype(bf)
    onesb = np.ones((128, 128), f32).astype(bf)
    Umat = (jj[:, None] <= jj[None, :]).astype(f32)
    onesf = np.ones((128, 128), f32)
    inv = (1.0 / (10000.0 ** (np.arange(0, 32, 2, dtype=np.float32) / 32.0))).astype(f32)
    maps = []
    for c in range(8):
        b, r = c // 4, c % 4
        h = r
        m = {}
        m["x"] = np.ascontiguousarray(x[b, TOK * r:TOK * (r + 1)])
        m["pos32"] = np.ascontiguousarray(np.broadcast_to(pos[b][None, :], (32, S)))
        kr = np.arange(o5, o5 + 32)
        kr_sw = np.concatenate([kr[16:], kr[:16]])
        colsA = np.concatenate([np.arange(h * 64, h * 64 + 64), np.arange(256 + h * 64, 256 + h * 64 + 64),
                                np.arange(o0 + h * 128, o0 + h * 128 + 128), np.arange(o1 + h * 128, o1 + h * 128 + 128),
                                np.array([o2 + h, o2 + 4 + h])])
        colsB = np.concatenate([np.arange(o3, o3 + 64), kr, np.arange(o3 + 64, o3 + 128), kr_sw,
                                np.arange(o3 + 128, o3 + 256), np.arange(o4, o4 + 128)])
        m["w_inA"] = np.ascontiguousarray(w_in[:, :, colsA])
        m["w_inB"] = np.ascontiguousarray(w_in[:, :, colsB])
        w_uq = np.asarray(inp["w_uq"], f32)
        cols = []
        for hh in (2 * r, 2 * r + 1):
            base = hh * 96
            nope = np.arange(base, base + 64)
            rope = np.arange(base + 64, base + 96)
            rope_sw = np.concatenate([rope[16:], rope[:16]])
            cols += [nope, rope, nope, rope_sw]
        wq = w_uq[:, :, np.concatenate(cols)]
        wq3 = np.zeros((NL, 3, 128, 384), f32)
        wq3[:, 0, 0:64] = wq[:, 0:64]
        wq3[:, 1, 0:64] = wq[:, 64:128]
        wq3[:, 2, :] = wq[:, 128:256]
        m["w_uq"] = wq3
        w_ukv = np.asarray(inp["w_ukv"], f32)
        cols = []
        for hh in (2 * r, 2 * r + 1):
            cols.append(np.arange(hh * 128, hh * 128 + 64))
        for hh in (2 * r, 2 * r + 1):
            cols.append(np.arange(hh * 128 + 64, hh * 128 + 128))
        m["w_ukv"] = np.ascontiguousarray(w_ukv[:, :, np.concatenate(cols)])
        rows = []
        for i in range(4):
            rows.append(np.arange(128 * i, 128 * i + 128))
            rows.append(np.arange(512 + 128 * i, 512 + 128 * i + 128))
        m["w_out"] = np.ascontiguousarray(np.asarray(inp["w_out"], f32)[:, np.concatenate(rows), :])
        m["w_up"] = np.asarray(inp["w_up"], f32)
        m["w_down"] = np.asarray(inp["w_down"], f32)
        pp = np.zeros((NL, 128, NPP), f32)
        for l in range(NL):
            pp[l, :, 0:8] = np.asarray(inp["norm_pre_mix"], f32)[l].reshape(8, 128).T
            pp[l, :, 8:16] = np.asarray(inp["norm_pre_mlp"], f32)[l].reshape(8, 128).T
            qn = np.asarray(inp["q_norm"], f32)[l]
            pp[l, 0:64, 16] = qn[0:64]
            pp[l, 0:64, 17] = qn[64:128]
            pp[l, :, 18] = qn[128:256]
            bg = np.asarray(inp["b_gates"], f32)[l]
            pp[l, :, 19] = bg[h]
            pp[l, :, 20] = bg[4 + h]
            cwt = np.asarray(inp["conv_w"], f32)[l]
            cb = np.asarray(inp["conv_b"], f32)[l]
            qc = np.arange(h * 64, h * 64 + 64)
            kc = np.arange(256 + h * 64, 256 + h * 64 + 64)
            pp[l, 0:64, 21:25] = cwt[:, qc].T
            pp[l, 0:64, 25] = cb[qc]
            pp[l, 0:64, 26:30] = cwt[:, kc].T
            pp[l, 0:64, 30] = cb[kc]
            pp[l, 64:96, 31] = np.concatenate([inv, inv])
            pp[l, 64:96, 32] = np.concatenate([-np.ones(16, f32), np.ones(16, f32)])
            pp[l, :, 33] = -math.pi
            pp[l, :, 34] = np.asarray(inp["kv_norm"], f32)[l]
        m["pp"] = pp
        gl = np.asarray(inp["ml_head_norm"], f32)[:, h * 128:(h + 1) * 128]
        m["gml"] = np.ascontiguousarray(np.broadcast_to(gl[:, None, :], (NL, 128, 128)))
        gp = np.stack([np.asarray(inp["norm_post_mix"], f32), np.asarray(inp["norm_post_mlp"], f32)], axis=1)
        m["gpost"] = np.ascontiguousarray(np.broadcast_to(gp[:, :, None, :], (NL, 2, 128, D)))
        m["identb"] = identb
        m["negmask"] = negmask
        m["onesb"] = onesb
        m["Umat"] = Umat
        m["onesf"] = onesf
        gidx = np.zeros((128, 8), np.int32)
        for i in range(4):
            gidx[:, 2 * i] = r * 1024 + i * 256 + jj
            gidx[:, 2 * i + 1] = r * 1024 + i * 256 + 128 + jj
        m["gidx"] = gidx
        maps.append(m)
    return maps


_NC_CACHE = {}


def kernel(**inputs):
    maps = _prep_inputs(inputs)
    if "nc" not in _NC_CACHE:
        _NC_CACHE["nc"] = build_program()
    nc = _NC_CACHE["nc"]
    res = run_bass_kernel_spmd(nc, maps, core_ids=list(range(8)))
    out = np.zeros((2, S, D), np.float32)
    for c in range(8):
        b, r = c // 4, c % 4
        out[b, TOK * r:TOK * (r + 1)] = np.asarray(res.results[c]["out"], np.float32)
    return out
```

```python
import contextlib
import math

import ml_dtypes
import numpy as np

import concourse.bass as bass
import concourse.mybir as mybir
from concourse.bass_utils import run_bass_kernel_spmd

F32 = mybir.dt.float32
BF16 = mybir.dt.bfloat16
I32 = mybir.dt.int32
AF = mybir.ActivationFunctionType
ALU = mybir.AluOpType

D = 1024
S = 8192
NL = 2
TOK = 2048
NBLK = 16
DFF = 4096
EPS = 1e-6
NPP = 40
QSCALE = 96 ** -0.5
LN8 = math.log(0.125)
GROUPS = [[0, 1, 2, 3], [4, 5, 6, 7]]
ARENA = 121344
ARENA0 = 22752
CR = 4


class _Op:
    __slots__ = ("eng", "fn", "deps", "dma_key", "inc", "signal", "token", "nobar")

    def __init__(self, eng, fn, dma_key, inc):
        self.eng = eng
        self.fn = fn
        self.deps = []
        self.dma_key = dma_key
        self.inc = inc
        self.signal = False
        self.token = None


class Prog:
    ENGS = ("pe", "act", "dve", "pool", "sp")

    def __init__(self, nc):
        self.nc = nc
        self.ops = {e: [] for e in self.ENGS}
        self.lastw = {}
        self.readers = {}
        self.dma_cnt = {}
        self.pending = {e: [] for e in self.ENGS}
        self.dmas_since = []

    def add(self, eng, fn, r=(), w=(), dma_key=None, inc=16, nobar=False):
        op = _Op(eng, fn, dma_key, inc)
        op.nobar = nobar
        deps = {}
        for x in r:
            p = self.lastw.get(x)
            if p is not None:
                deps[id(p)] = (p, True)
        for x in w:
            p = self.lastw.get(x)
            if p is not None and id(p) not in deps:
                deps[id(p)] = (p, False)
            for rd in self.readers.get(x, ()):
                if id(rd) not in deps:
                    deps[id(rd)] = (rd, False)
        for x in r:
            lst = self.readers.setdefault(x, [])
            if dma_key is None:
                lst[:] = [o for o in lst if not (o.dma_key is None and o.eng == eng)]
            lst.append(op)
        for x in w:
            self.lastw[x] = op
            self.readers[x] = []
        for p in self.pending[eng]:
            if id(p) not in deps:
                deps[id(p)] = (p, True)
        self.pending[eng] = []
        for dep, raw in deps.values():
            if dep is op:
                continue
            if dep.dma_key is not None:
                op.deps.append(dep)
            elif dep.eng != eng:
                dep.signal = True
                op.deps.append(dep)
            elif eng != "pe":
                dep.signal = True
                op.deps.append(dep)
        if dma_key is not None:
            c = self.dma_cnt.get(dma_key, 0) + inc
            self.dma_cnt[dma_key] = c
            op.token = (("dma", dma_key), c)
            if not nobar:
                self.dmas_since.append(op)
        self.ops[eng].append(op)
        return op

    def barrier(self):
        last = []
        for e in self.ENGS:
            for o in reversed(self.ops[e]):
                if not o.nobar:
                    last.append(o)
                    break
        allp = last + self.dmas_since
        self.dmas_since = []
        for e in self.ENGS:
            self.pending[e] = list(allp)

    def emit(self, stack):
        nc = self.nc
        sems = {}
        for e in self.ENGS:
            sems[("eng", e)] = stack.enter_context(nc.semaphore("s_" + e))
        for i, k in enumerate(self.dma_cnt):
            sems[("dma", k)] = stack.enter_context(nc.semaphore("d%d" % i))
        for e in self.ENGS:
            n = 0
            for op in self.ops[e]:
                if op.dma_key is None and op.signal:
                    n += 1
                    op.token = (("eng", e), n)
        block = stack.enter_context(nc.Block())
        handles = {"pe": block.tensor, "act": block.scalar, "dve": block.vector,
                   "pool": block.gpsimd, "sp": block.sync}
        for e in self.ENGS:
            ops = self.ops[e]

            def body(eng, ops=ops):
                waited = {}
                for op in ops:
                    need = {}
                    for d in op.deps:
                        k, v = d.token
                        if v > waited.get(k, 0) and v > need.get(k, 0):
                            need[k] = v
                    for k, v in need.items():
                        eng.wait_ge(sems[k], v)
                        waited[k] = v
                    ins = op.fn(eng)
                    if op.dma_key is not None:
                        ins.then_inc(sems[op.token[0]], op.inc)
                    elif op.signal:
                        ins.then_inc(sems[op.token[0]], 1)
                final = {}
                for op in ops:
                    if op.dma_key is not None:
                        k, v = op.token
                        final[k] = max(final.get(k, 0), v)
                for k, v in final.items():
                    if v > waited.get(k, 0):
                        eng.wait_ge(sems[k], v)
                        waited[k] = v

            handles[e](body)


class Arena:
    def __init__(self, ap, nbytes):
        self.ap = ap
        self.nbytes = nbytes
        self.off = 0

    def reset(self):
        self.off = 0

    def alloc(self, shape, dtype):
        item = 4 if dtype in (F32, I32) else 2
        n = 1
        for s_ in shape:
            n *= s_
        sz = (n * item + 31) // 32 * 32
        assert self.off + sz <= self.nbytes, ("arena overflow", self.off, sz)
        a = self.ap[:, self.off // 4:(self.off + sz) // 4]
        self.off += sz
        if dtype != F32:
            a = a.bitcast(dtype)
        a = a[:, 0:n]
        if len(shape) == 2:
            a = a.rearrange("p (a b) -> p a b", a=shape[0], b=shape[1])
        elif len(shape) == 3:
            a = a.rearrange("p (a b c) -> p a b c", a=shape[0], b=shape[1], c=shape[2])
        return a


class _Stop(Exception):
    pass


def build_program(dbg=(), maxstage=99, sim=False, cut=99, gcut=99):
    nc = bass.Bass("TRN2", target_bir_lowering=False)

    def din(name, shape, dt):
        return nc.dram_tensor(name, shape, dt, kind="ExternalInput").ap()

    x_d = din("x", [TOK, D], F32)
    pos_d = din("pos32", [32, S], I32)
    winA_d = din("w_inA", [NL, D, 386], F32)
    winB_d = din("w_inB", [NL, D, 448], F32)
    wuq_d = din("w_uq", [NL, 3, 128, 384], F32)
    wukv_d = din("w_ukv", [NL, 128, 256], F32)
    wout_d = din("w_out", [NL, D, D], F32)
    wup_d = din("w_up", [NL, D, DFF], F32)
    wdn_d = din("w_down", [NL, DFF, D], F32)
    pp_d = din("pp", [NL, 128, NPP], F32)
    gml_d = din("gml", [NL, 128, 128], F32)
    gpost_d = din("gpost", [NL, 2, 128, D], F32)
    identb_d = din("identb", [128, 128], BF16)
    negmask_d = din("negmask", [128, 128], BF16)
    onesb_d = din("onesb", [128, 128], BF16)
    U_d = din("Umat", [128, 128], F32)
    onesf_d = din("onesf", [128, 128], F32)
    gidx_d = din("gidx", [128, 8], I32)
    out_d = nc.dram_tensor("out", [TOK, D], F32, kind="ExternalOutput").ap()
    dbg_d = {}
    for name, shape, dt in dbg:
        dbg_d[name] = nc.dram_tensor(name, shape, dt, kind="ExternalOutput").ap()

    agin = nc.dram_tensor("agin", [4, D, 512], BF16)
    if sim:
        agout = nc.dram_tensor("agout", [4, 4 * D, 512], BF16, kind="ExternalInput")
    else:
        agout = nc.dram_tensor("agout", [4, 4 * D, 512], BF16)
    hsend = nc.dram_tensor("hsend", [4, 256, TOK], BF16)
    if sim:
        hall = nc.dram_tensor("hall", [4, 1024, TOK], BF16, kind="ExternalInput")
    else:
        hall = nc.dram_tensor("hall", [4, 1024, TOK], BF16)
    ropeC = nc.dram_tensor("ropeC", [32, S], F32)
    ropeS = nc.dram_tensor("ropeS", [32, S], F32)

    st = contextlib.ExitStack()
    with st:
        def T(n, s_, d_):
            return st.enter_context(nc.sbuf_tensor(n, s_, d_))

        X = T("X", [128, NBLK, D], F32)
        identb = T("identb_s", [128, 128], BF16)
        negmask = T("negmask_s", [128, 128], BF16)
        onesb = T("onesb_s", [128, 128], BF16)
        Um = T("U_s", [128, 128], F32)
        onesf = T("onesf_s", [128, 128], F32)
        gidx = T("gidx_s", [128, 8], I32)
        pp = T("pp_s", [128, NL, NPP], F32)
        gml = T("gml_s", [128, NL, 128], F32)
        nbf = T("nbf_s", [128, NL], F32)
        arena_t = T("arena", [128, (ARENA + ARENA0) // 4], F32)
        ps = st.enter_context(nc.psum_tensor("ps", [128, 8, 512], F32))
        ar = Arena(arena_t[:, 0:ARENA // 4], ARENA)
        ar0 = Arena(arena_t[:, ARENA // 4:(ARENA + ARENA0) // 4], ARENA0)
        P = Prog(nc)
        A = P.add

        bank_ctr = [0]

        def nb(pool=(0, 1, 2, 3, 4, 5)):
            b = pool[bank_ctr[0] % len(pool)]
            bank_ctr[0] += 1
            return b

        def PS(b):
            return ("ps", b)

        for i, (dst, src) in enumerate([(identb, identb_d), (negmask, negmask_d), (onesb, onesb_d),
                                         (Um, U_d), (onesf, onesf_d), (gidx, gidx_d)]):
            A("sp", lambda e, dst=dst, src=src: e.dma_start(out=dst[:], in_=src), w=[("c", i)], dma_key=("const", i))
        A("sp", lambda e: e.dma_start(out=pp[:], in_=pp_d.rearrange("l p n -> p l n")), w=["pp"], dma_key=("const", "pp"))
        A("sp", lambda e: e.dma_start(out=gml[:], in_=gml_d.rearrange("l p n -> p l n")), w=["gml"], dma_key=("const", "gml"))
        CONSTS = [("c", i) for i in range(6)] + ["pp", "gml"]
        xv = x_d.rearrange("(i p) d -> p i d", p=128)
        for j in range(4):
            A("sp", lambda e, j=j: e.dma_start(out=X[:, 4 * j:4 * j + 4, :], in_=xv[:, 4 * j:4 * j + 4, :]),
              w=[("X", i) for i in range(4 * j, 4 * j + 4)], dma_key=("x", j))
        A("dve", lambda e: e.tensor_scalar(out=nbf[:], in0=pp[:, :, 20], scalar1=-1.0, scalar2=None, op0=ALU.mult),
          r=["pp"], w=["nbf"])

        def dbg_dump(name, src_ap, res):
            if name in dbg_d:
                A("sp", lambda e: e.dma_start(out=dbg_d[name], in_=src_ap), r=res, dma_key=("dbg", name))

        ar0.reset()
        aTs = ar0.alloc([2, 8, 512], BF16)
        junk0 = ar0.alloc([D], BF16)
        ssq0 = ar0.alloc([NBLK], F32)
        rsd0 = ar0.alloc([NBLK], F32)
        abf0 = ar0.alloc([2, D], BF16)

        def norm_T(blocks, dst_fn, ssq=ssq0, rsd=rsd0, abf=abf0, junk=junk0):
            for ii, i in enumerate(blocks):
                sl = ii % 2
                A("act", lambda e, i=i: e.activation(out=junk[:], in_=X[:, i, :], func=AF.Square, accum_out=ssq[:, i:i + 1], scale=1.0),
                  r=[("X", i)], w=["junk", ("ssq", i)])
                A("act", lambda e, i=i: e.activation(out=rsd[:, i:i + 1], in_=ssq[:, i:i + 1], func=AF.Ln, bias=EPS, scale=1.0 / D),
                  r=[("ssq", i)], w=[("rsd", i)])
                A("act", lambda e, i=i: e.activation(out=rsd[:, i:i + 1], in_=rsd[:, i:i + 1], func=AF.Exp, scale=-0.5),
                  r=[("rsd", i)], w=[("rsd", i)])
                A("dve", lambda e, i=i, sl=sl: e.tensor_scalar(out=abf[:, sl, :], in0=X[:, i, :], scalar1=rsd[:, i:i + 1], scalar2=None, op0=ALU.mult),
                  r=[("X", i), ("rsd", i)], w=[("abf", id(abf), sl)])
                for half in range(2):
                    b = nb()
                    for c4 in range(4):
                        c = half * 4 + c4
                        A("pe", lambda e, b=b, c4=c4, c=c, sl=sl: e.matmul(ps[:, b, c4 * 128:(c4 + 1) * 128], lhsT=abf[:, sl, c * 128:(c + 1) * 128],
                                                                         rhs=identb[:], start=True, stop=True),
                          r=[("abf", id(abf), sl), ("c", 0)], w=[PS(b)])
                    dst, res = dst_fn(ii, half)
                    if half == 0:
                        A("act", lambda e, b=b, dst=dst: e.activation(out=dst, in_=ps[:, b, :].rearrange("p (c t) -> p c t", c=4), func=AF.Copy, scale=1.0),
                          r=[PS(b)], w=[res])
                    else:
                        A("dve", lambda e, b=b, dst=dst: e.tensor_copy(out=dst, in_=ps[:, b, :].rearrange("p (c t) -> p c t", c=4)),
                          r=[PS(b)], w=[res])

        aginv = agin.ap().rearrange("q (c p) t -> q p c t", p=128)

        def emit_P0(l, q4s):
            for q4 in q4s:
                sl = q4 % 2
                norm_T(range(4 * q4, 4 * q4 + 4),
                       lambda ii, half, sl=sl: (aTs[:, sl, half * 4:(half + 1) * 4, ii * 128:(ii + 1) * 128], ("aTs", sl, ii, half)))
                A("sp", lambda e, q4=q4, sl=sl: e.dma_start(out=aginv[q4], in_=aTs[:, sl]),
                  r=[("aTs", sl, ii, h) for ii in range(4) for h in range(2)], w=[("agin", q4)], dma_key=("agin", sl))
                if not sim:
                    A("pool", lambda e, q4=q4: e.collective_compute("AllGather", ALU.bypass, replica_groups=GROUPS,
                                                                    ins=[agin.ap()[q4].opt()], outs=[agout.ap()[q4].opt()]),
                      r=[("agin", q4)], w=[("agout", q4)], dma_key=("cc1", q4), inc=1, nobar=True)
            if l == 0 and list(q4s)[-1] == 3:
                dbg_dump("d_agout", agout.ap().rearrange("q r t -> (q r) t"), [("agout", q) for q in range(4)])

        if maxstage >= 0:
            emit_P0(0, range(4))

        ar.reset()
        RW = 2048
        posi = ar.alloc([RW], I32)
        posf = ar.alloc([RW], F32)
        ang = ar.alloc([RW], F32)
        ang2 = ar.alloc([RW], F32)
        tabc = ar.alloc([2, RW], F32)
        tabs = ar.alloc([2, RW], F32)
        rk = ar.alloc([RW], F32)
        rki = ar.alloc([RW], I32)
        rr1 = ar.alloc([RW], F32)
        CW1 = 6.28125
        CW2 = 2.0 * math.pi - 6.28125
        PI_SAFE = 3.1415925
        R6 = slice(64, 96)
        inv_ap = pp[R6, 0, 31:32]
        sgn_ap = pp[R6, 0, 32:33]
        TWO_PI = 2.0 * math.pi
        for n in range(S // RW):
            sl = n % 2
            A("sp", lambda e, n=n: e.dma_start(out=posi[R6, :], in_=pos_d[:, n * RW:(n + 1) * RW]),
              w=["posi"], dma_key="posi")
            A("dve", lambda e: e.tensor_copy(out=posf[R6, :], in_=posi[R6, :]), r=["posi"], w=["posf"])
            A("dve", lambda e: e.tensor_scalar(out=ang[R6, :], in0=posf[R6, :], scalar1=inv_ap, scalar2=None, op0=ALU.mult),
              r=["posf", "pp"], w=["ang"])
            for (tab, nm, shift) in ((tabs, "tabs", 0.0), (tabc, "tabc", 0.5 * math.pi)):
                src = ang
                if shift != 0.0:
                    A("dve", lambda e, shift=shift: e.tensor_scalar(out=ang2[R6, :], in0=ang[R6, :], scalar1=shift, scalar2=None, op0=ALU.add),
                      r=["ang"], w=["ang2"])
                    src = ang2
                sres = "ang" if shift == 0.0 else "ang2"
                A("dve", lambda e, src=src: e.tensor_scalar(out=rk[R6, :], in0=src[R6, :], scalar1=1.0 / TWO_PI, scalar2=None, op0=ALU.mult),
                  r=[sres], w=["rk"])
                A("dve", lambda e: e.tensor_copy(out=rki[R6, :], in_=rk[R6, :]), r=["rk"], w=["rki"])
                A("dve", lambda e: e.tensor_copy(out=rk[R6, :], in_=rki[R6, :]), r=["rki"], w=["rk"])
                A("dve", lambda e, src=src: e.scalar_tensor_tensor(out=rr1[R6, :], in0=rk[R6, :], scalar=-CW1, in1=src[R6, :], op0=ALU.mult, op1=ALU.add),
                  r=["rk", sres], w=["rr1"])
                A("dve", lambda e: e.scalar_tensor_tensor(out=rr1[R6, :], in0=rk[R6, :], scalar=-CW2, in1=rr1[R6, :], op0=ALU.mult, op1=ALU.add),
                  r=["rk", "rr1"], w=["rr1"])
                A("dve", lambda e: e.tensor_scalar(out=rr1[R6, :], in0=rr1[R6, :], scalar1=-PI_SAFE, scalar2=PI_SAFE, op0=ALU.max, op1=ALU.min),
                  r=["rr1"], w=["rr1"])
                A("act", lambda e, tab=tab, sl=sl: e.activation(out=tab[R6, sl, :], in_=rr1[R6, :], func=AF.Sin, scale=1.0),
                  r=["rr1"], w=[(nm, sl)])
            A("dve", lambda e, sl=sl: e.tensor_scalar(out=tabs[R6, sl, :], in0=tabs[R6, sl, :], scalar1=sgn_ap, scalar2=None, op0=ALU.mult),
              r=[("tabs", sl), "pp"], w=[("tabs", sl)])
            A("sp", lambda e, n=n, sl=sl: e.dma_start(out=ropeC.ap()[:, n * RW:(n + 1) * RW], in_=tabc[R6, sl, :]),
              r=[("tabc", sl)], w=["ropeC"], dma_key=("rc", sl))
            A("sp", lambda e, n=n, sl=sl: e.dma_start(out=ropeS.ap()[:, n * RW:(n + 1) * RW], in_=tabs[R6, sl, :]),
              r=[("tabs", sl)], w=["ropeS"], dma_key=("rs", sl))
        P.barrier()

        def chk(l, ph):
            if l * 10 + ph > maxstage:
                raise _Stop()

        try:
          for l in range(NL):
            chk(l, 0)
            chk(l, 1)
            ar.reset()
            winA = ar.alloc([8, 386], BF16)
            aT = ar.alloc([2, 8, 512], BF16)
            xq = ar.alloc([2, 516], F32)
            xk = ar.alloc([2, 516], F32)
            cvq = ar.alloc([512], F32)
            cvk = ar.alloc([512], F32)
            eq = ar.alloc([2, 512], F32)
            Qml = ar.alloc([S], BF16)
            Kml = ar.alloc([S], BF16)
            vaug = ar.alloc([64, 129], BF16)
            osig = ar.alloc([64, 128], BF16)
            eo = ar.alloc([4, 128], F32)
            graw = ar.alloc([64, 2], F32)
            igt = ar.alloc([64], F32)
            logf = ar.alloc([64], F32)
            bboth = ar.alloc([128], F32)
            bsb = bboth[:, 0:64]
            bend = bboth[:, 64:128]
            gcol = ar.alloc([64], F32)
            dcol = ar.alloc([64], F32)
            wj = ar.alloc([64], F32)
            ebt = ar.alloc([64], F32)
            tmp64 = ar.alloc([64], F32)
            UL = ar.alloc([2, 128], F32)
            DwT = ar.alloc([3, 128], F32)
            sT = ar.alloc([3, 128], BF16)
            H1sb = ar.alloc([2, 129], F32)
            hn = ar.alloc([2, 129], F32)
            sm = ar.alloc([2, 8], F32)
            tg = ar.alloc([2, 128], F32)
            hout = ar.alloc([2, 128], BF16)
            kw = ar.alloc([2, 64], BF16)
            Cst = ar.alloc([129], F32)
            Cprev = ar.alloc([CR, 129], BF16)
            hTst = ar.alloc([2, 512], BF16)
            junk2 = ar.alloc([128], F32)

            winAv = winA_d[l].rearrange("(c p) n -> p c n", p=128)
            A("pool", lambda e, winAv=winAv: e.dma_start(out=winA[:], in_=winAv), w=["winA"], dma_key="winA")
            for c in range(8):
                A("dve", lambda e, c=c, l=l: e.tensor_scalar(out=winA[:, c, :], in0=winA[:, c, :], scalar1=pp[:, l, c:c + 1], scalar2=None, op0=ALU.mult),
                  r=["winA", "pp"], w=["winA"])
            A("dve", lambda e: e.memset(vaug[:, :, 128:129], 1.0), w=["vaug1"])
            A("dve", lambda e: e.memset(xq[0:64, 1, 512:516], 0.0), w=[("xq", 1)])
            A("dve", lambda e: e.memset(xk[0:64, 1, 512:516], 0.0), w=[("xk", 1)])
            A("dve", lambda e: e.memset(Cst[0:64, :], 0.0), w=["Cst"])
            A("dve", lambda e: e.memset(Cprev[0:64, 0, :], 0.0), w=[("Cprev", 0)])
            cw = lambda col, l=l: pp[0:64, l, col:col + 1]
            for n in range(16):
                sl = n % 2
                q = n // 4
                A("sp", lambda e, n=n, sl=sl, q=q: e.dma_start(
                    out=aT[:, sl], in_=agout.ap()[n % 4, q * D:(q + 1) * D, :].rearrange("(c p) t -> p c t", p=128)),
                  r=[("agout", n % 4)], w=[("aT", sl)], dma_key=("aT", sl))
                gbank = {}
                for (col0, nm) in ((0, "xq"), (64, "xk")):
                    b = nb()
                    gbank[nm] = b
                    for c in range(8):
                        A("pe", lambda e, b=b, c=c, sl=sl, col0=col0: e.matmul(ps[0:64, b, :], lhsT=winA[:, c, col0:col0 + 64], rhs=aT[:, sl, c, :],
                                                                            start=(c == 0), stop=(c == 7)),
                          r=["winA", ("aT", sl)], w=[PS(b)])
                tmb = []
                for bl in range(4):
                    b = nb()
                    tmb.append(b)
                    for c in range(8):
                        A("pe", lambda e, b=b, c=c, sl=sl, bl=bl: e.matmul(ps[:, b, 0:258], lhsT=aT[:, sl, c, bl * 128:(bl + 1) * 128], rhs=winA[:, c, 128:386],
                                                                          start=(c == 0), stop=(c == 7)),
                          r=["winA", ("aT", sl)], w=[PS(b)])
                for (xb, nm) in ((xq, "xq"), (xk, "xk")):
                    b = gbank[nm]
                    A("act", lambda e, b=b, xb=xb, sl=sl: e.activation(out=xb[0:64, sl, 3:515], in_=ps[0:64, b, :], func=AF.Copy, scale=1.0),
                      r=[PS(b)], w=[(nm, sl, "m")])
                for (xb, cv, pc, nm) in ((xq, cvq, 21, "q"), (xk, cvk, 26, "k")):
                    A("dve", lambda e, xb=xb, sl=sl: e.tensor_copy(out=xb[0:64, sl, 0:3], in_=xb[0:64, 1 - sl, 512:515]),
                      r=[("x" + nm, 1 - sl, "m"), ("x" + nm, 1 - sl)], w=[("x" + nm, sl, "h")])
                    rr = [("x" + nm, sl, "m"), ("x" + nm, sl, "h"), "pp"]
                    A("dve", lambda e, xb=xb, cv=cv, sl=sl, s1=cw(pc + 3), s2_=cw(pc + 4): e.tensor_scalar(out=cv[0:64, :], in0=xb[0:64, sl, 3:515], scalar1=s1, scalar2=s2_,
                                                                                                          op0=ALU.mult, op1=ALU.add),
                      r=rr, w=[("cv", nm)])
                    for k in range(3):
                        A("dve", lambda e, xb=xb, cv=cv, sl=sl, k=k, s1=cw(pc + k): e.scalar_tensor_tensor(out=cv[0:64, :], in0=xb[0:64, sl, k:k + 512], scalar=s1,
                                                                                                          in1=cv[0:64, :], op0=ALU.mult, op1=ALU.add),
                          r=rr + [("cv", nm)], w=[("cv", nm)])
                    ei = 0 if nm == "q" else 1
                    A("act", lambda e, cv=cv, ei=ei: e.activation(out=eq[0:64, ei, :], in_=cv[0:64, :], func=AF.Exp, scale=-1.0),
                      r=[("cv", nm)], w=[("eq", ei)])
                for bl in range(4):
                    kb = 4 * n + bl
                    b = tmb[bl]
                    es = kb % 4
                    A("act", lambda e, b=b, kb=kb: e.activation(out=vaug[:, kb, 0:128], in_=ps[:, b, 0:128], func=AF.Copy, scale=1.0),
                      r=[PS(b)], w=[("vaug", kb)])
                    A("act", lambda e, b=b, es=es: e.activation(out=eo[:, es, :], in_=ps[:, b, 128:256], func=AF.Exp, scale=-1.0),
                      r=[PS(b)], w=[("eo", es)])
                    A("act", lambda e, b=b, kb=kb: e.activation(out=graw[:, kb, :], in_=ps[:, b, 256:258], func=AF.Copy, scale=1.0), r=[PS(b)], w=[("graw", kb)])
                for (cv, dst, nm) in ((cvq, Qml, "q"), (cvk, Kml, "k")):
                    ei = 0 if nm == "q" else 1
                    A("dve", lambda e, ei=ei: e.tensor_scalar(out=eq[0:64, ei, :], in0=eq[0:64, ei, :], scalar1=1.0, scalar2=None, op0=ALU.add),
                      r=[("eq", ei)], w=[("eq", ei)])
                    A("dve", lambda e, ei=ei: e.reciprocal(out=eq[0:64, ei, :], in_=eq[0:64, ei, :]), r=[("eq", ei)], w=[("eq", ei)])
                    A("dve", lambda e, cv=cv, ei=ei, dst=dst, n=n: e.tensor_tensor(out=dst[0:64, n * 512:(n + 1) * 512], in0=cv[0:64, :], in1=eq[0:64, ei, :], op=ALU.mult),
                      r=[("cv", nm), ("eq", ei)], w=[(nm + "ml", n)])
                for bl in range(4):
                    kb = 4 * n + bl
                    es = kb % 4
                    A("dve", lambda e, es=es: e.tensor_scalar(out=eo[:, es, :], in0=eo[:, es, :], scalar1=1.0, scalar2=None, op0=ALU.add),
                      r=[("eo", es)], w=[("eo", es)])
                    A("dve", lambda e, es=es, kb=kb: e.reciprocal(out=osig[:, kb, :], in_=eo[:, es, :]), r=[("eo", es)], w=[("osig", kb)])
            chk(l, 2)
            GR = [("graw", kb) for kb in range(64)]
            if gcut >= 1:
                A("dve", lambda e, l=l: e.tensor_scalar(out=igt[:], in0=graw[:, :, 0], scalar1=pp[:, l, 19:20], scalar2=None, op0=ALU.add),
                  r=GR + ["pp"], w=["igt"])
            if gcut >= 2:
                A("act", lambda e, l=l: e.activation(out=tmp64[:], in_=graw[:, :, 1], func=AF.Exp, bias=nbf[:, l:l + 1], scale=-1.0),
                  r=GR + ["nbf"], w=["tmp64"])
            if gcut >= 3:
                A("act", lambda e: e.activation(out=tmp64[:], in_=tmp64[:], func=AF.Ln, bias=1.0, scale=1.0), r=["tmp64"], w=["tmp64"])
            if gcut >= 4:
                A("dve", lambda e: e.tensor_scalar(out=logf[:], in0=tmp64[:], scalar1=-1.0, scalar2=None, op0=ALU.mult), r=["tmp64"], w=["logf"])
            bg = nb()
            if gcut >= 5:
                A("pe", lambda e, bg=bg: e.matmul(ps[:, bg, 0:64], lhsT=Um[:], rhs=logf[:], start=True, stop=True), r=["logf", ("c", 3)], w=[PS(bg)])
            if gcut >= 6:
                A("pe", lambda e, bg=bg: e.matmul(ps[:, bg, 64:128], lhsT=onesf[:], rhs=logf[:], start=True, stop=True), r=["logf", ("c", 4)], w=[PS(bg)])
            if gcut >= 7:
                A("dve", lambda e, bg=bg: e.tensor_copy(out=bboth[:], in_=ps[:, bg, 0:128]), r=[PS(bg)], w=["bsb"])
            if gcut >= 8:
                A("dve", lambda e: e.scalar_tensor_tensor(out=gcol[:], in0=igt[:], scalar=LN8, in1=bsb, op0=ALU.add, op1=ALU.subtract),
                  r=["igt", "bsb"], w=["gcol"])
            if gcut >= 9:
                A("act", lambda e: e.activation(out=dcol[:], in_=bend, func=AF.Exp, scale=1.0), r=["bsb"], w=["dcol"])
            if gcut >= 10:
                A("dve", lambda e: e.tensor_tensor(out=tmp64[:], in0=bend, in1=gcol[:], op=ALU.add), r=["gcol", "bsb"], w=["tmp64"])
            if gcut >= 11:
                A("act", lambda e: e.activation(out=wj[:], in_=tmp64[:], func=AF.Exp, scale=1.0), r=["tmp64"], w=["wj"])
            if gcut >= 12:
                A("act", lambda e: e.activation(out=ebt[:], in_=bsb, func=AF.Exp, scale=1.0), r=["bsb"], w=["ebt"])

            hsv = hsend.ap()
            def cbank(c):
                return c % 2

            BCL, BDT, BH1, BHT = 2, 3, 4, 7

            def A1(c):
                s2 = c % 2
                s3 = c % 3
                cs = slice(c * 128, (c + 1) * 128)
                n = c // 4
                cb = cbank(c)
                A("pe", lambda e: e.matmul(ps[:, cb, 0:64], lhsT=Kml[0:64, cs], rhs=identb[0:64, 0:64], start=True, stop=True),
                  r=[("kml", n), ("c", 0)], w=[PS(cb)])
                A("dve", lambda e: e.tensor_scalar(out=UL[:, s2, :], in0=Um[:], scalar1=logf[:, c:c + 1], scalar2=None, op0=ALU.mult),
                  r=["logf", ("c", 3)], w=[("UL", s2)])
                A("pe", lambda e: e.matmul(ps[:, BDT, 0:128], lhsT=onesf[:], rhs=UL[:, s2, :], start=True, stop=False),
                  r=[("UL", s2), ("c", 4)], w=[PS(BDT)])
                A("pe", lambda e: e.matmul(ps[:, BDT, 0:128], lhsT=identb[:], rhs=negmask[:], start=False, stop=True),
                  r=[("c", 0), ("c", 1)], w=[PS(BDT)])
                A("act", lambda e: e.activation(out=DwT[:, s3, :], in_=ps[:, BDT, 0:128], func=AF.Exp, bias=gcol[:, c:c + 1], scale=1.0),
                  r=[PS(BDT), "gcol"], w=[("DwT", s3)])
                A("pe", lambda e: e.matmul(ps[:, cb, 64:192], lhsT=Kml[0:64, cs], rhs=Qml[0:64, cs], start=True, stop=True),
                  r=[("qml", n), ("kml", n)], w=[PS(cb)])

            def A2(c):
                s2 = c % 2
                s3 = c % 3
                cb = cbank(c)
                A("dve", lambda e: e.tensor_scalar(out=kw[:, s2, :], in0=ps[:, cb, 0:64], scalar1=wj[:, c:c + 1], scalar2=None, op0=ALU.mult),
                  r=[PS(cb), "wj"], w=[("kw", s2)])
                A("dve", lambda e: e.tensor_tensor(out=sT[:, s3, :], in0=ps[:, cb, 64:192], in1=DwT[:, s3, :], op=ALU.mult),
                  r=[PS(cb), ("DwT", s3)], w=[("sT", s3)])
                A("pe", lambda e: e.matmul(ps[0:64, BCL, 0:129], lhsT=kw[:, s2, :], rhs=vaug[:, c, :], start=True, stop=True),
                  r=[("kw", s2), ("vaug", c), "vaug1"], w=[PS(BCL)])

            def A3(c):
                cb = cbank(c)
                A("dve", lambda e: e.tensor_scalar(out=Cst[0:64, :], in0=Cst[0:64, :], scalar1=dcol[0:64, c:c + 1], scalar2=None, op0=ALU.mult),
                  r=["Cst", "dcol"], w=["Cst"])
                A("dve", lambda e: e.tensor_tensor(out=Cst[0:64, :], in0=ps[0:64, BCL, 0:129], in1=Cst[0:64, :], op=ALU.add),
                  r=[PS(BCL), "Cst"], w=["Cst"])
                if c < 63:
                    A("act", lambda e: e.activation(out=Cprev[0:64, (c + 1) % CR, :], in_=Cst[0:64, :], func=AF.Copy, scale=1.0),
                      r=["Cst"], w=[("Cprev", (c + 1) % CR)])

            def B1(c):
                s2 = c % 2
                s3 = c % 3
                cs = slice(c * 128, (c + 1) * 128)
                n = c // 4
                bh2 = 5 + c % 2
                A("pe", lambda e: e.matmul(ps[:, BH1, 0:129], lhsT=sT[:, s3, :], rhs=vaug[:, c, :], start=True, stop=True),
                  r=[("sT", s3), ("vaug", c), "vaug1"], w=[PS(BH1)])
                A("pe", lambda e: e.matmul(ps[:, bh2, 0:129], lhsT=Qml[0:64, cs], rhs=Cprev[0:64, c % CR, :], start=True, stop=True),
                  r=[("qml", n), ("Cprev", c % CR)], w=[PS(bh2)])
                A("act", lambda e: e.activation(out=H1sb[:, s2, :], in_=ps[:, BH1, 0:129], func=AF.Copy, scale=1.0), r=[PS(BH1)], w=[("H1sb", s2)])

            def B2(c):
                s2 = c % 2
                bh2 = 5 + c % 2
                A("dve", lambda e: e.scalar_tensor_tensor(out=hn[:, s2, :], in0=ps[:, bh2, 0:129], scalar=ebt[:, c:c + 1], in1=H1sb[:, s2, :],
                                                          op0=ALU.mult, op1=ALU.add),
                  r=[PS(bh2), ("H1sb", s2), "ebt"], w=[("hn", s2)])
                A("dve", lambda e: e.tensor_scalar(out=sm[:, s2, 0:1], in0=hn[:, s2, 128:129], scalar1=-1.0, scalar2=None, op0=ALU.mult),
                  r=[("hn", s2)], w=[("sm0", s2)])
                A("dve", lambda e: e.scalar_tensor_tensor(out=sm[:, s2, 1:2], in0=hn[:, s2, 128:129], scalar=1.0, in1=sm[:, s2, 0:1], op0=ALU.max, op1=ALU.max),
                  r=[("hn", s2), ("sm0", s2)], w=[("sm1", s2)])
                A("dve", lambda e: e.reciprocal(out=sm[:, s2, 2:3], in_=sm[:, s2, 1:2]), r=[("sm1", s2)], w=[("sm2", s2)])
                A("act", lambda e: e.activation(out=junk2[:], in_=hn[:, s2, 0:128], func=AF.Square, scale=sm[:, s2, 2:3], accum_out=sm[:, s2, 3:4]),
                  r=[("hn", s2), ("sm2", s2)], w=["junk2", ("sm3", s2)])
                A("act", lambda e: e.activation(out=sm[:, s2, 4:5], in_=sm[:, s2, 3:4], func=AF.Ln, bias=EPS, scale=1.0 / 128), r=[("sm3", s2)], w=[("sm4", s2)])
                A("act", lambda e: e.activation(out=sm[:, s2, 5:6], in_=sm[:, s2, 4:5], func=AF.Exp, scale=-0.5), r=[("sm4", s2)], w=[("sm5", s2)])

            def B3(c):
                s2 = c % 2
                A("dve", lambda e: e.tensor_tensor(out=sm[:, s2, 6:7], in0=sm[:, s2, 2:3], in1=sm[:, s2, 5:6], op=ALU.mult),
                  r=[("sm2", s2), ("sm5", s2)], w=[("sm6", s2)])
                A("dve", lambda e, l=l: e.tensor_tensor(out=tg[:, s2, :], in0=osig[:, c, :], in1=gml[:, l, :], op=ALU.mult),
                  r=[("osig", c), "gml"], w=[("tg", s2)])
                A("dve", lambda e: e.scalar_tensor_tensor(out=hout[:, s2, :], in0=hn[:, s2, 0:128], scalar=sm[:, s2, 6:7], in1=tg[:, s2, :], op0=ALU.mult, op1=ALU.mult),
                  r=[("hn", s2), ("sm6", s2), ("tg", s2)], w=[("hout", s2)])
                A("pe", lambda e: e.matmul(ps[:, BHT, 0:128], lhsT=hout[:, s2, :], rhs=identb[:], start=True, stop=True),
                  r=[("hout", s2), ("c", 0)], w=[PS(BHT)])

            def B4(c):
                hs = (c // 4) % 2
                A("act", lambda e: e.activation(out=hTst[:, hs, (c % 4) * 128:(c % 4 + 1) * 128], in_=ps[:, BHT, 0:128], func=AF.Copy, scale=1.0),
                  r=[PS(BHT)], w=[("hTst", hs, c % 4)])
                if c % 4 == 3:
                    A("sp", lambda e: e.dma_start(out=hsv[c // 16, 0:128, ((c // 4) % 4) * 512:((c // 4) % 4 + 1) * 512], in_=hTst[:, hs, :]),
                      r=[("hTst", hs, k) for k in range(4)], w=[("hsend", "ml", c // 4)], dma_key=("hT", hs))

            STG = ((A2, 1), (A1, 0), (B1, 2), (B2, 3), (B4, 5), (B3, 4), (A3, 1))
            NCH = 64 if cut >= 1 else 0
            for it in range(NCH + 5 if NCH else 0):
                for fn_, off_ in STG:
                    c_ = it - off_
                    if 0 <= c_ < NCH:
                        fn_(c_)
            P.barrier()

            chk(l, 3)
            ar.reset()
            winB = ar.alloc([8, 448], BF16)
            wuq = ar.alloc([3, 384], BF16)
            wukv = ar.alloc([256], BF16)
            aTb = ar.alloc([2, 8, 512], BF16)
            C2 = ar.alloc([2, 512], F32)
            S2 = ar.alloc([2, 512], F32)
            cqr = ar.alloc([3, 512], BF16)
            cqs = ar.alloc([3, 512], BF16)
            cqn = ar.alloc([3, 512], BF16)
            rbc = ar.alloc([512], F32)
            ckr = ar.alloc([512], BF16)
            cks = ar.alloc([512], BF16)
            ckn = ar.alloc([512], BF16)
            rbc2 = ar.alloc([512], F32)
            t1 = ar.alloc([512], F32)
            t2 = ar.alloc([512], F32)
            QT = ar.alloc([2, 2, 512], BF16)
            KT = ar.alloc([2, S], BF16)
            Vt = ar.alloc([64, 2, 65], BF16)
            PT = ar.alloc([3, 2, 512], BF16)
            Osb = ar.alloc([2, 512], F32)
            rden = t2
            hst = ar.alloc([2, 512], BF16)

            A("pool", lambda e, l=l: e.dma_start(out=winB[:], in_=winB_d[l].rearrange("(c p) n -> p c n", p=128)), w=["winB"], dma_key="winB")
            for c in range(8):
                A("dve", lambda e, c=c, l=l: e.tensor_scalar(out=winB[:, c, :], in0=winB[:, c, :], scalar1=pp[:, l, c:c + 1], scalar2=None, op0=ALU.mult),
                  r=["winB", "pp"], w=["winB"])
            A("pool", lambda e, l=l: e.dma_start(out=wuq[:], in_=wuq_d[l].rearrange("k p n -> p k n")), w=["wuq"], dma_key="wuq")
            for k in range(3):
                A("dve", lambda e, k=k, l=l: e.tensor_scalar(out=wuq[:, k, :], in0=wuq[:, k, :], scalar1=pp[:, l, 16 + k:17 + k], scalar2=None, op0=ALU.mult),
                  r=["wuq", "pp"], w=["wuq"])
            A("pool", lambda e, l=l: e.dma_start(out=wukv[:], in_=wukv_d[l]), w=["wukv"], dma_key="wukv")
            A("dve", lambda e, l=l: e.tensor_scalar(out=wukv[:], in0=wukv[:], scalar1=pp[:, l, 34:35], scalar2=None, op0=ALU.mult),
              r=["wukv", "pp"], w=["wukv"])
            A("dve", lambda e: e.memset(Vt[:, :, :, 64:65], 1.0), w=["Vt1"])
            KCH = ((64, 0), (64, 1), (128, 2))
            MISC = (4, 5)
            SPAIRS = ((0, 1), (2, 3))
            sp_ctr = [0]
            octr = [0]
            hsvv = hsend.ap()

            def proj_steps(G):
                sl = G % 2
                q = G // 4
                cs = slice(G * 512, (G + 1) * 512)
                st_ = {}

                def s0():
                    A("sp", lambda e: e.dma_start(
                        out=aTb[:, sl], in_=agout.ap()[G % 4, q * D:(q + 1) * D, :].rearrange("(c p) t -> p c t", p=128)),
                      r=[("agout", G % 4)], w=[("aTb", sl)], dma_key=("aTb", sl))
                    A("sp", lambda e: e.dma_start(out=C2[R6, sl, :], in_=ropeC.ap()[:, G * 512:(G + 1) * 512]), r=["ropeC"], w=[("C2", sl)], dma_key=("C2", sl))
                    A("sp", lambda e: e.dma_start(out=S2[R6, sl, :], in_=ropeS.ap()[:, G * 512:(G + 1) * 512]), r=["ropeS"], w=[("S2", sl)], dma_key=("S2", sl))

                def grp(gi, col0, M):
                    bnk = nb(MISC)
                    st_[("gb", gi)] = bnk
                    for c in range(8):
                        A("pe", lambda e, c=c: e.matmul(ps[0:M, bnk, :], lhsT=winB[:, c, col0:col0 + M], rhs=aTb[:, sl, c, :],
                                                       start=(c == 0), stop=(c == 7)),
                          r=["winB", ("aTb", sl)], w=[PS(bnk)])
                    return bnk

                def s1():
                    for gi, col0 in ((0, 0), (1, 96)):
                        bnk = grp(gi, col0, 96)
                        A("dve", lambda e, gi=gi, bnk=bnk: e.tensor_copy(out=cqr[0:64, gi, :], in_=ps[0:64, bnk, :]), r=[PS(bnk)], w=[("cqr", gi)])
                        if gi == 0:
                            A("dve", lambda e, bnk=bnk: e.tensor_tensor(out=t1[R6, :], in0=ps[R6, bnk, :], in1=C2[R6, sl, :], op=ALU.mult), r=[PS(bnk), ("C2", sl)], w=["t1"])
                        else:
                            A("dve", lambda e, bnk=bnk: e.tensor_tensor(out=t2[R6, :], in0=ps[R6, bnk, :], in1=S2[R6, sl, :], op=ALU.mult), r=[PS(bnk), ("S2", sl)], w=["t2"])
                        A("act", lambda e, gi=gi: e.activation(out=cqs[0:64, gi, :], in_=cqr[0:64, gi, :], func=AF.Square, scale=1.0), r=[("cqr", gi)], w=[("cqs", gi)])
                    A("dve", lambda e: e.tensor_tensor(out=KT[R6, 0, cs], in0=t1[R6, :], in1=t2[R6, :], op=ALU.add), r=["t1", "t2"], w=[("KTr", 0, G)])
                    A("dve", lambda e: e.tensor_copy(out=KT[R6, 1, cs], in_=KT[R6, 0, cs]), r=[("KTr", 0, G)], w=[("KTr", 1, G)])

                def s2():
                    bnk = grp(2, 192, 128)
                    A("dve", lambda e: e.tensor_copy(out=cqr[:, 2, :], in_=ps[:, bnk, :]), r=[PS(bnk)], w=[("cqr", 2)])
                    A("act", lambda e: e.activation(out=cqs[:, 2, :], in_=cqr[:, 2, :], func=AF.Square, scale=1.0), r=[("cqr", 2)], w=[("cqs", 2)])
                    bnk2 = grp(3, 320, 128)
                    A("dve", lambda e: e.tensor_copy(out=ckr[:], in_=ps[:, bnk2, :]), r=[PS(bnk2)], w=["ckr"])
                    A("act", lambda e: e.activation(out=cks[:], in_=ckr[:], func=AF.Square, scale=1.0), r=["ckr"], w=["cks"])

                def s3():
                    bq = nb(MISC)
                    for k, (rows, _) in enumerate(KCH):
                        A("pe", lambda e, k=k, rows=rows: e.matmul(ps[:, bq, :], lhsT=onesb[0:rows, :], rhs=cqs[0:rows, k, :], start=(k == 0), stop=(k == 2)),
                          r=[("cqs", k), ("c", 2)], w=[PS(bq)])
                    A("act", lambda e: e.activation(out=rbc[:], in_=ps[:, bq, :], func=AF.Ln, bias=EPS, scale=1.0 / 256), r=[PS(bq)], w=["rbc"])
                    A("act", lambda e: e.activation(out=rbc[:], in_=rbc[:], func=AF.Exp, scale=-0.5), r=["rbc"], w=["rbc"])
                    bk = nb(MISC)
                    A("pe", lambda e: e.matmul(ps[:, bk, :], lhsT=onesb[:], rhs=cks[:], start=True, stop=True), r=["cks", ("c", 2)], w=[PS(bk)])
                    A("act", lambda e: e.activation(out=rbc2[:], in_=ps[:, bk, :], func=AF.Ln, bias=EPS, scale=1.0 / 128), r=[PS(bk)], w=["rbc2"])
                    A("act", lambda e: e.activation(out=rbc2[:], in_=rbc2[:], func=AF.Exp, scale=-0.5), r=["rbc2"], w=["rbc2"])

                def s4():
                    for k, (rows, _) in enumerate(KCH):
                        A("dve", lambda e, k=k, rows=rows: e.tensor_tensor(out=cqn[0:rows, k, :], in0=cqr[0:rows, k, :], in1=rbc[0:rows, :], op=ALU.mult),
                          r=[("cqr", k), "rbc"], w=[("cqn", k)])
                    A("dve", lambda e: e.tensor_tensor(out=ckn[:], in0=ckr[:], in1=rbc2[:], op=ALU.mult), r=["ckr", "rbc2"], w=["ckn"])

                def qhead(hh):
                    br = nb(MISC)
                    for k, (rows, _) in enumerate(KCH):
                        A("pe", lambda e, k=k, rows=rows: e.matmul(ps[0:96, br, :], lhsT=wuq[0:rows, k, hh * 192:hh * 192 + 96], rhs=cqn[0:rows, k, :],
                                                                  start=(k == 0), stop=(k == 2)),
                          r=["wuq", ("cqn", k)], w=[PS(br)])
                    A("dve", lambda e: e.tensor_copy(out=QT[0:64, sl, hh, :], in_=ps[0:64, br, :]), r=[PS(br)], w=[("QTn", sl, hh)])
                    A("dve", lambda e: e.tensor_tensor(out=t1[R6, :], in0=ps[R6, br, :], in1=C2[R6, sl, :], op=ALU.mult), r=[PS(br), ("C2", sl)], w=["t1"])
                    bs = nb(MISC)
                    for k, (rows, _) in enumerate(KCH):
                        A("pe", lambda e, k=k, rows=rows: e.matmul(ps[0:96, bs, :], lhsT=wuq[0:rows, k, hh * 192 + 96:hh * 192 + 192], rhs=cqn[0:rows, k, :],
                                                                  start=(k == 0), stop=(k == 2)),
                          r=["wuq", ("cqn", k)], w=[PS(bs)])
                    A("dve", lambda e: e.tensor_tensor(out=t2[R6, :], in0=ps[R6, bs, :], in1=S2[R6, sl, :], op=ALU.mult), r=[PS(bs), ("S2", sl)], w=["t2"])
                    A("dve", lambda e: e.tensor_tensor(out=QT[R6, sl, hh, :], in0=t1[R6, :], in1=t2[R6, :], op=ALU.add), r=["t1", "t2"], w=[("QTr", sl, hh)])

                def s5():
                    qhead(0)

                def s6():
                    qhead(1)

                def s7():
                    for hh in range(2):
                        bnk = nb(MISC)
                        A("pe", lambda e, hh=hh, bnk=bnk: e.matmul(ps[0:64, bnk, :], lhsT=wukv[:, hh * 64:(hh + 1) * 64], rhs=ckn[:], start=True, stop=True),
                          r=["wukv", "ckn"], w=[PS(bnk)])
                        A("act", lambda e, hh=hh, bnk=bnk: e.activation(out=KT[0:64, hh, cs], in_=ps[0:64, bnk, :], func=AF.Copy, scale=1.0), r=[PS(bnk)], w=[("KTn", hh, G)])
                    bv = nb(MISC)
                    for bl in range(4):
                        A("pe", lambda e, bl=bl: e.matmul(ps[:, bv, bl * 128:(bl + 1) * 128], lhsT=ckn[:, bl * 128:(bl + 1) * 128], rhs=wukv[:, 128:256], start=True, stop=True),
                          r=["wukv", "ckn"], w=[PS(bv)])
                    A("dve", lambda e: e.tensor_copy(out=Vt[:, 4 * G:4 * G + 4, :, 0:64], in_=ps[:, bv, :].rearrange("p (b h d) -> p b h d", b=4, h=2)),
                      r=[PS(bv)], w=[("Vt", G)])

                return [s0, s1, s2, s3, s4, s5, s6, s7]

            pend = []

            def attention(G, hh, extra):
                sl = G % 2
                ob = (6, 7)[octr[0] % 2]
                octr[0] += 1
                Qr = [("QTn", sl, hh), ("QTr", sl, hh)]
                nkb = 4 * G + 4
                units = [(kb, kb + 1) for kb in range(0, 4 * G, 2)] + [(kb,) for kb in range(4 * G, nkb)]
                info = {}

                def qk(ui):
                    u = units[ui]
                    b0, b1 = SPAIRS[sp_ctr[0] % 2]
                    sp_ctr[0] += 1
                    res = ("ps2", b0)
                    c0 = 0
                    for j, kb in enumerate(u):
                        bnk = (b0, b1)[j]
                        kg = kb // 4
                        Kr = [("KTn", hh, kg), ("KTr", hh, kg)]
                        ks = slice(kb * 128, (kb + 1) * 128)
                        a_ = kb - 4 * G
                        if a_ < 0:
                            A("pe", lambda e, bnk=bnk, ks=ks: e.matmul(ps[:, bnk, :], lhsT=KT[0:96, hh, ks], rhs=QT[0:96, sl, hh, :], start=True, stop=True),
                              r=Kr + Qr, w=[res])
                        else:
                            c0 = a_ * 128
                            A("pe", lambda e, bnk=bnk, ks=ks, c0=c0: e.matmul(ps[:, bnk, c0:c0 + 128], lhsT=KT[0:96, hh, ks], rhs=QT[0:96, sl, hh, c0:c0 + 128],
                                                                           start=True, stop=False),
                              r=Kr + Qr, w=[res])
                            A("pe", lambda e, bnk=bnk, c0=c0: e.matmul(ps[:, bnk, c0:c0 + 128], lhsT=identb[:], rhs=negmask[:], start=False, stop=True),
                              r=[("c", 0), ("c", 1)], w=[res])
                            if a_ < 3:
                                A("pe", lambda e, bnk=bnk, ks=ks, c0=c0: e.matmul(ps[:, bnk, c0 + 128:512], lhsT=KT[0:96, hh, ks], rhs=QT[0:96, sl, hh, c0 + 128:512],
                                                                               start=True, stop=True),
                                  r=Kr + Qr, w=[res])
                    info[ui] = (b0, res, c0)

                def pv(ui):
                    u = units[ui]
                    b0, res, c0 = info[ui]
                    p3 = ui % 3
                    if len(u) == 2:
                        A("act", lambda e: e.activation(out=PT[:, p3, :, :], in_=ps[:, b0:b0 + 2, :], func=AF.Exp, scale=QSCALE),
                          r=[res], w=[("PT", p3)])
                    else:
                        A("act", lambda e: e.activation(out=PT[:, p3, 0, c0:512], in_=ps[:, b0, c0:512], func=AF.Exp, scale=QSCALE),
                          r=[res], w=[("PT", p3)])
                    for j, kb in enumerate(u):
                        kg = kb // 4
                        A("pe", lambda e, j=j, kb=kb: e.matmul(ps[0:65, ob, c0:512], lhsT=Vt[:, kb, hh, :], rhs=PT[:, p3, j, c0:512],
                                                               start=(kb == 0), stop=(kb == nkb - 1)),
                          r=[("PT", p3), ("Vt", kg), "Vt1"], w=[PS(ob)])

                nu = len(units)
                for it in range(nu + 1):
                    if it < nu:
                        qk(it)
                    if it >= 1:
                        pv(it - 1)
                    if it == min(2, nu) and pend:
                        pend.pop(0)()
                    if extra and ((it == 0 and hh == 0) or (it >= 4 and it % 4 == 0)):
                        extra.pop(0)()
                A("act", lambda e: e.activation(out=Osb[0:65, hh, :], in_=ps[0:65, ob, :], func=AF.Copy, scale=1.0), r=[PS(ob)], w=[("Osb", hh)])
                A("dve", lambda e: e.reciprocal(out=Osb[64:65, hh, :], in_=Osb[64:65, hh, :]), r=[("Osb", hh)], w=[("Osb", hh)])

                def epilogue():
                    bb = nb(MISC)
                    A("pe", lambda e: e.matmul(ps[0:64, bb, :], lhsT=onesf[64:65, 0:64], rhs=Osb[64:65, hh, :], start=True, stop=True), r=[("Osb", hh), ("c", 4)], w=[PS(bb)])
                    A("dve", lambda e: e.tensor_tensor(out=hst[0:64, hh, :], in0=ps[0:64, bb, :], in1=Osb[0:64, hh, :], op=ALU.mult),
                      r=[("Osb", hh), PS(bb)], w=[("hst", hh)])
                    A("sp", lambda e: e.dma_start(out=hsvv[G // 4, 128 + 64 * hh:192 + 64 * hh, (G % 4) * 512:(G % 4 + 1) * 512], in_=hst[0:64, hh, :]),
                      r=[("hst", hh)], w=[("hsend", "mla", hh, G)], dma_key=("hst", hh))

                pend.append(epilogue)

            for stp in proj_steps(0):
                stp()
            for G in range(16):
                extra = proj_steps(G + 1) if G + 1 < 16 else []
                for hh in range(2):
                    attention(G, hh, extra)
                while extra:
                    extra.pop(0)()
                if G % 4 == 3 or G == 15:
                    while pend:
                        pend.pop(0)()
                if G % 4 == 3 and not sim:
                    kq = G // 4
                    A("pool", lambda e, kq=kq: e.collective_compute("AllGather", ALU.bypass, replica_groups=GROUPS,
                                                                    ins=[hsend.ap()[kq].opt()], outs=[hall.ap()[kq].opt()]),
                      r=[("hsend", "ml", n_) for n_ in range(4 * kq, 4 * kq + 4)] + [("hsend", "mla", h_, g_) for h_ in range(2) for g_ in range(4 * kq, 4 * kq + 4)],
                      w=[("hall", kq)], dma_key=("cc2", kq), inc=1)
            P.barrier()

            chk(l, 4)
            ar.reset()
            hcat = ar.alloc([8, TOK], BF16)
            wo = ar.alloc([8, D], BF16)
            gpm = ar.alloc([D], F32)
            tt = ar.alloc([2, D], F32)
            st4 = ar.alloc([NBLK, 4], F32)
            junk4 = ar.alloc([512], BF16)
            A("pool", lambda e, l=l: e.dma_start(out=wo[:], in_=wout_d[l].rearrange("(c p) n -> p c n", p=128)), w=["wo"], dma_key="wo")
            A("sp", lambda e, l=l: e.dma_start(out=gpm[:], in_=gpost_d[l, 0]), w=["gpm"], dma_key="gpm")
            HALL = [("hall", kq) for kq in range(4)]
            if l == 0:
                dbg_dump("d_hall", hall.ap().rearrange("q r t -> (q r) t"), HALL)
            hallv = hall.ap().rearrange("q r t -> (q r) t")
            for k in range(8):
                A("pool", lambda e, k=k: e.indirect_dma_start(out=hcat[:, k, :], out_offset=None, in_=hallv,
                                                              in_offset=bass.IndirectOffsetOnAxis(ap=gidx[:, k:k + 1], axis=0)),
                  r=HALL + [("c", 5)], w=[("hcat", k)], dma_key=("hcat", k))

            def post_norm(i, srcs, gtile, l=l, final=False, st4=st4, junk4=junk4, tt=tt, frompsum=False):
                s2 = i % 2
                for half, (src, res) in enumerate(srcs):
                    A("act", lambda e, src=src, half=half: e.activation(out=junk4[:], in_=src, func=AF.Square, accum_out=st4[:, i, half:half + 1], scale=1.0),
                      r=[res], w=["junk4", ("st4", i, half)])
                A("dve", lambda e: e.tensor_tensor(out=st4[:, i, 2:3], in0=st4[:, i, 0:1], in1=st4[:, i, 1:2], op=ALU.add),
                  r=[("st4", i, 0), ("st4", i, 1)], w=[("st4", i, 2)])
                A("act", lambda e: e.activation(out=st4[:, i, 3:4], in_=st4[:, i, 2:3], func=AF.Ln, bias=EPS, scale=1.0 / D), r=[("st4", i, 2)], w=[("st4", i, 3)])
                A("act", lambda e: e.activation(out=st4[:, i, 3:4], in_=st4[:, i, 3:4], func=AF.Exp, scale=-0.5), r=[("st4", i, 3)], w=[("st4", i, 3)])
                for half, (src, res) in enumerate(srcs):
                    if frompsum:
                        A("act", lambda e, src=src, half=half: e.activation(out=tt[:, s2, half * 512:(half + 1) * 512], in_=src, func=AF.Copy, scale=1.0),
                          r=[res], w=[("tt", s2, half)])
                        A("dve", lambda e, half=half: e.tensor_tensor(out=tt[:, s2, half * 512:(half + 1) * 512], in0=tt[:, s2, half * 512:(half + 1) * 512],
                                                                       in1=gtile[:, half * 512:(half + 1) * 512], op=ALU.mult),
                          r=[("tt", s2, half), "gtile"], w=[("tt", s2, half)])
                    else:
                        A("dve", lambda e, src=src, half=half: e.tensor_tensor(out=tt[:, s2, half * 512:(half + 1) * 512], in0=src, in1=gtile[:, half * 512:(half + 1) * 512], op=ALU.mult),
                          r=[res, "gtile"], w=[("tt", s2, half)])
                A("dve", lambda e: e.scalar_tensor_tensor(out=X[:, i, :], in0=tt[:, s2, :], scalar=st4[:, i, 3:4], in1=X[:, i, :], op0=ALU.mult, op1=ALU.add),
                  r=[("tt", s2, 0), ("tt", s2, 1), ("st4", i, 3), ("X", i)], w=[("X", i)])

            A("dve", lambda e: e.tensor_copy(out=gpm[:, 0:1], in_=gpm[:, 0:1]), r=["gpm"], w=["gtile"])
            for i in range(NBLK):
                bh = [nb(), nb()]
                for half in range(2):
                    for k in range(8):
                        A("pe", lambda e, b=bh[half], k=k, i=i, half=half: e.matmul(ps[:, b, :], lhsT=hcat[:, k, i * 128:(i + 1) * 128], rhs=wo[:, k, half * 512:(half + 1) * 512],
                                                                                 start=(k == 0), stop=(k == 7)),
                          r=[("hcat", k), "wo"], w=[PS(bh[half])])
                post_norm(i, [(ps[:, bh[0], :], PS(bh[0])), (ps[:, bh[1], :], PS(bh[1]))], gpm, frompsum=True)
            if l == 0:
                dbg_dump("d_xmix", X[:].rearrange("p i d -> p (i d)"), [("X", i) for i in range(NBLK)])
            P.barrier()

            chk(l, 5)
            ar.reset()
            mT = ar.alloc([8, 1024], BF16)
            ysb = ar.alloc([8, D], F32)
            hT = ar.alloc([2, 4, 1024], BF16)
            wu = ar.alloc([2, 8, 512], BF16)
            wd = ar.alloc([2, 4, D], BF16)
            h1 = ar.alloc([3, 512], BF16)
            gpl = ar.alloc([D], F32)
            tt = ar.alloc([2, D], F32)
            st4 = ar.alloc([NBLK, 4], F32)
            junk4 = ar.alloc([512], BF16)
            A("sp", lambda e, l=l: e.dma_start(out=gpl[:], in_=gpost_d[l, 1]), w=["gpl"], dma_key="gpl")
            A("dve", lambda e: e.tensor_copy(out=gpl[:, 0:1], in_=gpl[:, 0:1]), r=["gpl"], w=["gtile"])
            outv = out_d.rearrange("(i p) d -> p i d", p=128)
            h1c = 0
            for tgp in range(2):
                blocks = range(8 * tgp, 8 * tgp + 8)
                norm_T(blocks, lambda ii, half: (mT[:, half * 4:(half + 1) * 4, ii * 128:(ii + 1) * 128], ("mT", ii, half)))
                MT = [("mT", ii, h) for ii in range(8) for h in range(2)]
                for s in range(8):
                    ws = s % 2
                    A("pool", lambda e, s=s, ws=ws, l=l: e.dma_start(out=wu[:, ws], in_=wup_d[l][:, s * 512:(s + 1) * 512].rearrange("(c p) f -> p c f", p=128)),
                      w=[("wu", ws)], dma_key=("wu", ws))
                    for c in range(8):
                        A("dve", lambda e, c=c, ws=ws, l=l: e.tensor_scalar(out=wu[:, ws, c, :], in0=wu[:, ws, c, :], scalar1=pp[:, l, 8 + c:9 + c], scalar2=None, op0=ALU.mult),
                          r=[("wu", ws), "pp"], w=[("wu", ws)])
                    A("pool", lambda e, s=s, ws=ws, l=l: e.dma_start(out=wd[:, ws], in_=wdn_d[l][s * 512:(s + 1) * 512, :].rearrange("(c p) n -> p c n", p=128)),
                      w=[("wd", ws)], dma_key=("wd", ws))
                    for fc in range(4):
                        for th in range(2):
                            b = nb()
                            for c in range(8):
                                A("pe", lambda e, b=b, c=c, ws=ws, fc=fc, th=th: e.matmul(ps[:, b, :], lhsT=wu[:, ws, c, fc * 128:(fc + 1) * 128], rhs=mT[:, c, th * 512:(th + 1) * 512],
                                                                                      start=(c == 0), stop=(c == 7)),
                                  r=[("wu", ws)] + MT, w=[PS(b)])
                            hs_ = h1c % 3
                            h1c += 1
                            A("act", lambda e, b=b, hs_=hs_: e.activation(out=h1[:, hs_, :], in_=ps[:, b, :], func=AF.Relu, scale=1.0), r=[PS(b)], w=[("h1", hs_)])
                            A("dve", lambda e, hs_=hs_, ws=ws, fc=fc, th=th: e.tensor_tensor(out=hT[:, ws, fc, th * 512:(th + 1) * 512], in0=h1[:, hs_, :], in1=h1[:, hs_, :], op=ALU.mult),
                              r=[("h1", hs_)], w=[("hT", ws, fc, th)])
                    for bl in range(8):
                        for half in range(2):
                            b = nb()
                            for fc in range(4):
                                A("pe", lambda e, b=b, fc=fc, ws=ws, bl=bl, half=half: e.matmul(ps[:, b, :], lhsT=hT[:, ws, fc, bl * 128:(bl + 1) * 128], rhs=wd[:, ws, fc, half * 512:(half + 1) * 512],
                                                                                             start=(fc == 0), stop=(fc == 3)),
                                  r=[("hT", ws, fc, bl // 4), ("wd", ws)], w=[PS(b)])
                            ydst = ysb[:, bl, half * 512:(half + 1) * 512]
                            if s == 0:
                                A("act", lambda e, b=b, ydst=ydst: e.activation(out=ydst, in_=ps[:, b, :], func=AF.Copy, scale=1.0), r=[PS(b)], w=[("ysb", bl, half)])
                            else:
                                A("dve", lambda e, b=b, ydst=ydst: e.tensor_tensor(out=ydst, in0=ps[:, b, :], in1=ydst, op=ALU.add), r=[PS(b), ("ysb", bl, half)], w=[("ysb", bl, half)])
                for bl in range(8):
                    i = 8 * tgp + bl
                    post_norm(i, [(ysb[:, bl, 0:512], ("ysb", bl, 0)), (ysb[:, bl, 512:1024], ("ysb", bl, 1))], gpl, st4=st4, junk4=junk4, tt=tt)
                    if l == NL - 1:
                        A("sp", lambda e, i=i: e.dma_start(out=outv[:, i, :], in_=X[:, i, :]), r=[("X", i)], dma_key=("out", i % 4))
                if l < NL - 1 and l * 10 + 10 <= maxstage:
                    emit_P0(l + 1, range(2 * tgp, 2 * tgp + 2))
            P.barrier()

        except _Stop:
            P.barrier()
            outv = out_d.rearrange("(i p) d -> p i d", p=128)
            for i in range(NBLK):
                A("sp", lambda e, i=i: e.dma_start(out=outv[:, i, :], in_=X[:, i, :]), r=[("X", i)], dma_key=("out", i % 4))

        with nc.allow_low_precision(reason="bf16 matmul operands by design; all accumulation/statistics in fp32"):
            P.emit(st)
    return nc


def _prep_inputs(inp):
    f32 = np.float32
    bf = ml_dtypes.bfloat16
    x = np.asarray(inp["x"], f32)
    pos = np.asarray(inp["positions"]).astype(np.int32)
    w_in = np.asarray(inp["w_in"], f32)
    o0, o1, o2, o3, o4, o5 = 512, 1024, 1536, 1544, 1800, 1928
    jj = np.arange(128)
    identb = np.eye(128, dtype=f32).astype(bf)
    negmask = np.where(jj[None, :] >= jj[:, None], 0.0, -30000.0).astype(f32).astype(bf)
    onesb = np.ones((128, 128), f32).astype(bf)
    Umat = (jj[:, None] <= jj[None, :]).astype(f32)
    onesf = np.ones((128, 128), f32)
    inv = (1.0 / (10000.0 ** (np.arange(0, 32, 2, dtype=np.float32) / 32.0))).astype(f32)
    maps = []
    for c in range(8):
        b, r = c // 4, c % 4
        h = r
        m = {}
        m["x"] = np.ascontiguousarray(x[b, TOK * r:TOK * (r + 1)])
        m["pos32"] = np.ascontiguousarray(np.broadcast_to(pos[b][None, :], (32, S)))
        kr = np.arange(o5, o5 + 32)
        kr_sw = np.concatenate([kr[16:], kr[:16]])
        colsA = np.concatenate([np.arange(h * 64, h * 64 + 64), np.arange(256 + h * 64, 256 + h * 64 + 64),
                                np.arange(o0 + h * 128, o0 + h * 128 + 128), np.arange(o1 + h * 128, o1 + h * 128 + 128),
                                np.array([o2 + h, o2 + 4 + h])])
        colsB = np.concatenate([np.arange(o3, o3 + 64), kr, np.arange(o3 + 64, o3 + 128), kr_sw,
                                np.arange(o3 + 128, o3 + 256), np.arange(o4, o4 + 128)])
        m["w_inA"] = np.ascontiguousarray(w_in[:, :, colsA])
        m["w_inB"] = np.ascontiguousarray(w_in[:, :, colsB])
        w_uq = np.asarray(inp["w_uq"], f32)
        cols = []
        for hh in (2 * r, 2 * r + 1):
            base = hh * 96
            nope = np.arange(base, base + 64)
            rope = np.arange(base + 64, base + 96)
            rope_sw = np.concatenate([rope[16:], rope[:16]])
            cols += [nope, rope, nope, rope_sw]
        wq = w_uq[:, :, np.concatenate(cols)]
        wq3 = np.zeros((NL, 3, 128, 384), f32)
        wq3[:, 0, 0:64] = wq[:, 0:64]
        wq3[:, 1, 0:64] = wq[:, 64:128]
        wq3[:, 2, :] = wq[:, 128:256]
        m["w_uq"] = wq3
        w_ukv = np.asarray(inp["w_ukv"], f32)
        cols = []
        for hh in (2 * r, 2 * r + 1):
            cols.append(np.arange(hh * 128, hh * 128 + 64))
        for hh in (2 * r, 2 * r + 1):
            cols.append(np.arange(hh * 128 + 64, hh * 128 + 128))
        m["w_ukv"] = np.ascontiguousarray(w_ukv[:, :, np.concatenate(cols)])
        rows = []
        for i in range(4):
            rows.append(np.arange(128 * i, 128 * i + 128))
            rows.append(np.arange(512 + 128 * i, 512 + 128 * i + 128))
        m["w_out"] = np.ascontiguousarray(np.asarray(inp["w_out"], f32)[:, np.concatenate(rows), :])
        m["w_up"] = np.asarray(inp["w_up"], f32)
        m["w_down"] = np.asarray(inp["w_down"], f32)
        pp = np.zeros((NL, 128, NPP), f32)
        for l in range(NL):
            pp[l, :, 0:8] = np.asarray(inp["norm_pre_mix"], f32)[l].reshape(8, 128).T
            pp[l, :, 8:16] = np.asarray(inp["norm_pre_mlp"], f32)[l].reshape(8, 128).T
            qn = np.asarray(inp["q_norm"], f32)[l]
            pp[l, 0:64, 16] = qn[0:64]
            pp[l, 0:64, 17] = qn[64:128]
            pp[l, :, 18] = qn[128:256]
            bg = np.asarray(inp["b_gates"], f32)[l]
            pp[l, :, 19] = bg[h]
            pp[l, :, 20] = bg[4 + h]
            cwt = np.asarray(inp["conv_w"], f32)[l]
            cb = np.asarray(inp["conv_b"], f32)[l]
            qc = np.arange(h * 64, h * 64 + 64)
            kc = np.arange(256 + h * 64, 256 + h * 64 + 64)
            pp[l, 0:64, 21:25] = cwt[:, qc].T
            pp[l, 0:64, 25] = cb[qc]
            pp[l, 0:64, 26:30] = cwt[:, kc].T
            pp[l, 0:64, 30] = cb[kc]
            pp[l, 64:96, 31] = np.concatenate([inv, inv])
            pp[l, 64:96, 32] = np.concatenate([-np.ones(16, f32), np.ones(16, f32)])
            pp[l, :, 33] = -math.pi
            pp[l, :, 34] = np.asarray(inp["kv_norm"], f32)[l]
        m["pp"] = pp
        gl = np.asarray(inp["ml_head_norm"], f32)[:, h * 128:(h + 1) * 128]
        m["gml"] = np.ascontiguousarray(np.broadcast_to(gl[:, None, :], (NL, 128, 128)))
        gp = np.stack([np.asarray(inp["norm_post_mix"], f32), np.asarray(inp["norm_post_mlp"], f32)], axis=1)
        m["gpost"] = np.ascontiguousarray(np.broadcast_to(gp[:, :, None, :], (NL, 2, 128, D)))
        m["identb"] = identb
        m["negmask"] = negmask
        m["onesb"] = onesb
        m["Umat"] = Umat
        m["onesf"] = onesf
        gidx = np.zeros((128, 8), np.int32)
        for i in range(4):
            gidx[:, 2 * i] = r * 1024 + i * 256 + jj
            gidx[:, 2 * i + 1] = r * 1024 + i * 256 + 128 + jj
        m["gidx"] = gidx
        maps.append(m)
    return maps


_NC_CACHE = {}


def kernel(**inputs):
    maps = _prep_inputs(inputs)
    if "nc" not in _NC_CACHE:
        _NC_CACHE["nc"] = build_program()
    nc = _NC_CACHE["nc"]
    res = run_bass_kernel_spmd(nc, maps, core_ids=list(range(8)))
    out = np.zeros((2, S, D), np.float32)
    for c in range(8):
        b, r = c // 4, c % 4
        out[b, TOK * r:TOK * (r + 1)] = np.asarray(res.results[c]["out"], np.float32)
    return out
```
